# Optimizing a Trainium2 kernel written in Bass

```python
import jax
import jax.numpy as jnp
from jax import lax
import numpy as np

D_MODEL = 1024
BATCH = 4
SEQ = 8192
DEPTH = 4

HEAD_DIM = 64
N_HEADS = D_MODEL // HEAD_DIM
SB_HEADS = N_HEADS // 2
NSA_HEADS = N_HEADS - SB_HEADS
NSA_KV_GROUPS = 2
FOX_HEADS = N_HEADS
CMP_LEN = 32
CMP_STRIDE = 16
SEL_LEN = 64
SEL_TOP = 16
WINDOW = 512
N_BRANCH = 3
Q_BLOCK = 128
ROPE_THETA = 500000.0
ROT_DIM = HEAD_DIM // 4
D_FF = 256 * ((8 * D_MODEL // 3 + 255) // 256)
CONV_WIDTH = 3
NORM_EPS = 1e-6
N_EVEN = (DEPTH + 1) // 2
N_ODD = DEPTH // 2
SB_W = SB_HEADS * HEAD_DIM
NSA_QW = NSA_HEADS * HEAD_DIM
NSA_KVW = NSA_KV_GROUPS * HEAD_DIM
EVEN_SPLITS = (SB_W, SB_W, SB_W, NSA_QW) + (NSA_KVW,) * 6 + (NSA_HEADS * N_BRANCH,)
EVEN_IN = sum(EVEN_SPLITS)
FOX_W = FOX_HEADS * HEAD_DIM
ODD_SPLITS = (FOX_W, FOX_W, FOX_W, FOX_HEADS)
ODD_IN = sum(ODD_SPLITS)

kernel_name = 'hybrid_stickbreak_nsa_fox_convffn'


def rmsnorm(x, g):
    x32 = x.astype(jnp.float32)
    y = x32 * lax.rsqrt(jnp.mean(x32 * x32, axis=-1, keepdims=True) + NORM_EPS)
    return y.astype(x.dtype) * g


def split_cols(h, sizes):
    offs = np.cumsum(sizes)[:-1].tolist()
    return jnp.split(h, offs, axis=-1)


def partial_rope(x, pos):
    half = ROT_DIM // 2
    inv_freq = ROPE_THETA ** (-(jnp.arange(half, dtype=jnp.float32) * 2.0 / ROT_DIM))
    ang = pos[:, None] * inv_freq[None, :]
    cos = jnp.cos(ang)[None, :, None, :].astype(x.dtype)
    sin = jnp.sin(ang)[None, :, None, :].astype(x.dtype)
    x1 = x[..., :half]
    x2 = x[..., half:ROT_DIM]
    return jnp.concatenate([x1 * cos - x2 * sin, x1 * sin + x2 * cos, x[..., ROT_DIM:]], axis=-1)


def masked_softmax(s, mask):
    s = jnp.where(mask, s, -jnp.inf)
    m = jnp.max(s, axis=-1, keepdims=True)
    m = jnp.where(jnp.isfinite(m), m, 0.0)
    e = jnp.where(mask, jnp.exp(s - m), 0.0)
    d = jnp.sum(e, axis=-1, keepdims=True)
    return e / jnp.where(d > 0, d, 1.0)


def sweep_query_blocks(fn, seq):
    out = lax.map(fn, jnp.arange(seq // Q_BLOCK))
    nb, bsz, q, nh, dh = out.shape
    return jnp.moveaxis(out, 0, 1).reshape(bsz, nb * q, nh, dh)


def stick_breaking_attention(q, k, v):
    bsz, seq, nh, dh = q.shape
    scale = dh ** -0.5
    kpos = jnp.arange(seq)

    def block(i):
        q0 = i * Q_BLOCK
        qb = lax.dynamic_slice_in_dim(q, q0, Q_BLOCK, axis=1)
        tpos = q0 + jnp.arange(Q_BLOCK)
        z = jnp.einsum('bqhd,bshd->bhqs', qb, k, preferred_element_type=jnp.float32) * scale
        mask = kpos[None, :] < tpos[:, None]
        log_keep = jnp.where(mask, jax.nn.log_sigmoid(-z), 0.0)
        after = lax.cumsum(log_keep, axis=3, reverse=True) - log_keep
        a = jnp.where(mask, jnp.exp(jax.nn.log_sigmoid(z) + after), 0.0)
        return jnp.einsum('bhqs,bshd->bqhd', a.astype(v.dtype), v)

    return sweep_query_blocks(block, seq)


def forgetting_attention(q, k, v, log_f):
    bsz, seq, nh, dh = q.shape
    scale = dh ** -0.5
    kpos = jnp.arange(seq)
    cum = jnp.transpose(jnp.cumsum(log_f, axis=1), (0, 2, 1))

    def block(i):
        q0 = i * Q_BLOCK
        qb = lax.dynamic_slice_in_dim(q, q0, Q_BLOCK, axis=1)
        cq = lax.dynamic_slice_in_dim(cum, q0, Q_BLOCK, axis=2)
        tpos = q0 + jnp.arange(Q_BLOCK)
        s = jnp.einsum('bqhd,bshd->bhqs', qb, k, preferred_element_type=jnp.float32) * scale
        s = s + (cq[..., :, None] - cum[..., None, :])
        p = masked_softmax(s, kpos[None, :] <= tpos[:, None])
        return jnp.einsum('bhqs,bshd->bqhd', p.astype(v.dtype), v)

    return sweep_query_blocks(block, seq)


def nsa_attention(q, k_cmp, v_cmp, k_slc, v_slc, k_win, v_win, gates, pos_k, pos_v, w_ck, w_cv):
    bsz, seq, nh, dh = q.shape
    ng = k_cmp.shape[2]
    rep = nh // ng
    scale = dh ** -0.5
    n_cmp = (seq - CMP_LEN) // CMP_STRIDE + 1
    c_start = jnp.arange(n_cmp) * CMP_STRIDE
    cidx = c_start[:, None] + jnp.arange(CMP_LEN)[None, :]

    def compress(t, pe, w):
        blocks = t[:, cidx] + pe[None, None, :, None, :]
        blocks = jnp.moveaxis(blocks, 3, 2).reshape(bsz, n_cmp, ng, CMP_LEN * dh)
        return blocks @ w

    cmp_end = c_start + CMP_LEN - 1
    kc = partial_rope(compress(k_cmp, pos_k, w_ck), cmp_end.astype(jnp.float32))
    vc = compress(v_cmp, pos_v, w_cv)
    n_sel = seq // SEL_LEN
    top = min(SEL_TOP, n_sel)
    ks = k_slc.reshape(bsz, n_sel, SEL_LEN, ng, dh).transpose(0, 3, 1, 2, 4)
    vs = v_slc.reshape(bsz, n_sel, SEL_LEN, ng, dh).transpose(0, 3, 1, 2, 4)
    sel_start = jnp.arange(n_sel) * SEL_LEN
    overlap = (c_start[:, None] < sel_start[None, :] + SEL_LEN) & (c_start[:, None] + CMP_LEN > sel_start[None, :])
    cmp_to_sel = overlap.astype(jnp.float32)
    blk = jnp.arange(n_sel)
    bi = jnp.arange(bsz)[:, None, None, None]
    gi = jnp.arange(ng)[None, :, None, None]
    kw = jnp.pad(k_win, ((0, 0), (WINDOW, 0), (0, 0), (0, 0)))
    vw = jnp.pad(v_win, ((0, 0), (WINDOW, 0), (0, 0), (0, 0)))

    def block(i):
        q0 = i * Q_BLOCK
        tpos = q0 + jnp.arange(Q_BLOCK)
        qb = lax.dynamic_slice_in_dim(q, q0, Q_BLOCK, axis=1).reshape(bsz, Q_BLOCK, ng, rep, dh)
        sc = jnp.einsum('bqgrd,bcgd->bgrqc', qb, kc, preferred_element_type=jnp.float32) * scale
        pc = masked_softmax(sc, cmp_end[None, :] <= tpos[:, None])
        o_cmp = jnp.einsum('bgrqc,bcgd->bqgrd', pc.astype(vc.dtype), vc)
        imp = jnp.einsum('bgrqc,cn->bgqn', pc, cmp_to_sel)
        cur = tpos // SEL_LEN
        causal = blk[None, :] <= cur[:, None]
        forced = (blk[None, :] == 0) | (blk[None, :] == cur[:, None]) | (blk[None, :] == cur[:, None] - 1)
        score = jnp.where(causal, jnp.where(forced, jnp.inf, imp), -jnp.inf)
        top_val, top_idx = lax.top_k(score, top)
        sel_ok = top_val > -jnp.inf
        kg = ks[bi, gi, top_idx]
        vg = vs[bi, gi, top_idx]
        kpos = top_idx[..., None] * SEL_LEN + jnp.arange(SEL_LEN)
        smask = sel_ok[..., None] & (kpos <= tpos[None, None, :, None, None])
        ss = jnp.einsum('bqgrd,bgqnld->bgrqnl', qb, kg, preferred_element_type=jnp.float32) * scale
        ps = masked_softmax(ss.reshape(bsz, ng, rep, Q_BLOCK, top * SEL_LEN),
                            smask.reshape(bsz, ng, 1, Q_BLOCK, top * SEL_LEN))
        ps = ps.reshape(bsz, ng, rep, Q_BLOCK, top, SEL_LEN)
        o_slc = jnp.einsum('bgrqnl,bgqnld->bqgrd', ps.astype(vg.dtype), vg)
        kwb = lax.dynamic_slice_in_dim(kw, q0, Q_BLOCK + WINDOW, axis=1)
        vwb = lax.dynamic_slice_in_dim(vw, q0, Q_BLOCK + WINDOW, axis=1)
        wpos = q0 - WINDOW + jnp.arange(Q_BLOCK + WINDOW)
        wmask = (wpos[None, :] >= 0) & (wpos[None, :] <= tpos[:, None]) & (wpos[None, :] > tpos[:, None] - WINDOW)
        sw = jnp.einsum('bqgrd,bkgd->bgrqk', qb, kwb, preferred_element_type=jnp.float32) * scale
        pw = masked_softmax(sw, wmask)
        o_win = jnp.einsum('bgrqk,bkgd->bqgrd', pw.astype(vwb.dtype), vwb)
        gb = lax.dynamic_slice_in_dim(gates, q0, Q_BLOCK, axis=1).reshape(bsz, Q_BLOCK, ng, rep, N_BRANCH)
        o = gb[..., 0:1] * o_cmp + gb[..., 1:2] * o_slc + gb[..., 2:3] * o_win
        return o.reshape(bsz, Q_BLOCK, nh, dh)

    return sweep_query_blocks(block, seq)


def even_mixer(h, w_in, pos_k, pos_v, w_ck, w_cv, w_out):
    bsz, seq, _ = h.shape
    sq, sk, sv, nq, kc, vc, ksl, vsl, kwn, vwn, g = split_cols(h @ w_in, EVEN_SPLITS)
    pos = jnp.arange(seq, dtype=jnp.float32)

    def heads(t):
        return t.reshape(bsz, seq, -1, HEAD_DIM)

    o_sb = stick_breaking_attention(heads(sq), heads(sk), heads(sv))
    o_nsa = nsa_attention(partial_rope(heads(nq), pos), heads(kc), heads(vc),
                          partial_rope(heads(ksl), pos), heads(vsl),
                          partial_rope(heads(kwn), pos), heads(vwn),
                          jax.nn.sigmoid(g.reshape(bsz, seq, NSA_HEADS, N_BRANCH)),
                          pos_k, pos_v, w_ck, w_cv)
    o = jnp.concatenate([o_sb.reshape(bsz, seq, SB_W), o_nsa.reshape(bsz, seq, NSA_QW)], axis=-1)
    return o @ w_out


def odd_mixer(h, w_in, b_f, w_out):
    bsz, seq, _ = h.shape
    q, k, v, f = split_cols(h @ w_in, ODD_SPLITS)
    log_f = jax.nn.log_sigmoid((f + b_f).astype(jnp.float32))
    o = forgetting_attention(q.reshape(bsz, seq, FOX_HEADS, HEAD_DIM),
                             k.reshape(bsz, seq, FOX_HEADS, HEAD_DIM),
                             v.reshape(bsz, seq, FOX_HEADS, HEAD_DIM), log_f)
    return o.reshape(bsz, seq, FOX_W) @ w_out


def conv_glu_ffn(h, w_in, conv_w, conv_b, w_out):
    a, b = jnp.split(h @ w_in, 2, axis=-1)
    taps = conv_w[:, None, :]
    a = lax.conv_general_dilated(a, taps, window_strides=(1,), padding=[(CONV_WIDTH - 1, 0)],
                                 dimension_numbers=('NWC', 'WIO', 'NWC'),
                                 feature_group_count=a.shape[-1]) + conv_b
    return (jax.nn.silu(a) * b) @ w_out


def setup_inputs(seed: int = 0) -> dict:
    key = jax.random.key(seed)
    k = jax.random.split(key, 17)

    def nrm(kk, shape, scale):
        return jax.random.normal(kk, shape, jnp.float32) * scale

    d = D_MODEL
    lc = CMP_LEN * HEAD_DIM
    return {
        'x': nrm(k[0], (BATCH, SEQ, d), 1.0),
        'attn_norm': 1.0 + nrm(k[1], (DEPTH, d), 0.02),
        'ffn_norm': 1.0 + nrm(k[2], (DEPTH, d), 0.02),
        'ev_w_in': nrm(k[3], (N_EVEN, d, EVEN_IN), d ** -0.5),
        'ev_cmp_pos_k': nrm(k[4], (N_EVEN, CMP_LEN, HEAD_DIM), 0.2),
        'ev_cmp_pos_v': nrm(k[5], (N_EVEN, CMP_LEN, HEAD_DIM), 0.2),
        'ev_cmp_w_k': nrm(k[6], (N_EVEN, lc, HEAD_DIM), lc ** -0.5),
        'ev_cmp_w_v': nrm(k[7], (N_EVEN, lc, HEAD_DIM), lc ** -0.5),
        'ev_w_out': nrm(k[8], (N_EVEN, SB_W + NSA_QW, d), (SB_W + NSA_QW) ** -0.5),
        'od_w_in': nrm(k[9], (N_ODD, d, ODD_IN), d ** -0.5),
        'od_b_f': 2.0 + nrm(k[10], (N_ODD, FOX_HEADS), 0.5),
        'od_w_out': nrm(k[11], (N_ODD, FOX_W, d), FOX_W ** -0.5),
        'ffn_w_in': nrm(k[12], (DEPTH, d, 2 * D_FF), d ** -0.5),
        'ffn_conv_w': nrm(k[13], (DEPTH, CONV_WIDTH, D_FF), CONV_WIDTH ** -0.5),
        'ffn_conv_b': nrm(k[14], (DEPTH, D_FF), 0.02),
        'ffn_w_out': nrm(k[15], (DEPTH, D_FF, d), D_FF ** -0.5),
        'final_norm': 1.0 + nrm(k[16], (d,), 0.02),
    }


def reference(x, attn_norm, ffn_norm, ev_w_in, ev_cmp_pos_k, ev_cmp_pos_v, ev_cmp_w_k, ev_cmp_w_v,
              ev_w_out, od_w_in, od_b_f, od_w_out, ffn_w_in, ffn_conv_w, ffn_conv_b, ffn_w_out,
              final_norm):
    for layer in range(DEPTH):
        h = rmsnorm(x, attn_norm[layer])
        if layer % 2 == 0:
            e = layer // 2
            x = x + even_mixer(h, ev_w_in[e], ev_cmp_pos_k[e], ev_cmp_pos_v[e],
                               ev_cmp_w_k[e], ev_cmp_w_v[e], ev_w_out[e])
        else:
            o = layer // 2
            x = x + odd_mixer(h, od_w_in[o], od_b_f[o], od_w_out[o])
        x = x + conv_glu_ffn(rmsnorm(x, ffn_norm[layer]), ffn_w_in[layer], ffn_conv_w[layer],
                             ffn_conv_b[layer], ffn_w_out[layer])
    return rmsnorm(x, final_norm)
```

```python
import ml_dtypes
import numpy as np
from contextlib import ExitStack
import concourse.bass as bass
import concourse.mybir as mybir
from concourse.bass_utils import run_bass_kernel_spmd

F32 = mybir.dt.float32
BF16 = mybir.dt.bfloat16
AF = mybir.ActivationFunctionType
ALU = mybir.AluOpType
AX = mybir.AxisListType

EPOCH = 16000
NDSEM = 8
SAME_ENGINE_SYNC = True


class Sched:
    ENGS = ['pe', 'dve', 'act', 'pool', 'sp']

    def __init__(self, nc, es):
        self.nc = nc
        self.es = es
        self.ops = {e: [] for e in self.ENGS}
        self.ncomp = {e: 0 for e in self.ENGS}
        self.ndma = {e: 0 for e in self.ENGS}
        self.last_w = {}
        self.readers = {}
        self.csem = {}
        self.dsem = {}
        self.bar = {}

    def barrier(self):
        toks = []
        for e in self.ENGS:
            if self.ncomp[e] > 0:
                toks.append(('c', e, self.ncomp[e] - 1))
            for i in range(max(0, self.ndma[e] - NDSEM), self.ndma[e]):
                toks.append(('d', e, i))
        for e in self.ENGS:
            self.bar[e] = set(toks) | (self.bar.get(e) or set())

    def _sem(self, name):
        return self.es.enter_context(self.nc.semaphore(name))

    def tok2sem(self, tok):
        kind, eng, idx = tok
        if kind == 'c':
            ep = idx // EPOCH
            key = (eng, ep)
            if key not in self.csem:
                self.csem[key] = self._sem(f"c_{eng}_{ep}")
            return self.csem[key], (idx % EPOCH) + 1
        else:
            slot = idx % NDSEM
            key = (eng, slot)
            if key not in self.dsem:
                self.dsem[key] = self._sem(f"d_{eng}_{slot}")
            return self.dsem[key], 16 * (idx // NDSEM + 1)

    def add(self, eng, fn, reads=(), writes=(), dma=False):
        deps = set()
        for k in reads:
            t = self.last_w.get(k)
            if t is not None:
                deps.add(t)
        for k in writes:
            t = self.last_w.get(k)
            if t is not None:
                deps.add(t)
            for r in self.readers.get(k, ()):
                deps.add(r)
        if self.bar.get(eng):
            deps |= self.bar[eng]
            self.bar[eng] = None
        if dma:
            tok = ('d', eng, self.ndma[eng])
            self.ndma[eng] += 1
        else:
            tok = ('c', eng, self.ncomp[eng])
            self.ncomp[eng] += 1
        deps.discard(tok)
        self.ops[eng].append((fn, deps, tok))
        for k in reads:
            self.readers.setdefault(k, []).append(tok)
        for k in writes:
            self.last_w[k] = tok
            self.readers[k] = []
        return tok

    def emit(self):
        nc = self.nc
        for e in self.ENGS:
            for fn, deps, tok in self.ops[e]:
                self.tok2sem(tok)
        final_toks = []
        for e in self.ENGS:
            if self.ops[e]:
                lastc = None
                dl = {}
                for fn, deps, tok in self.ops[e]:
                    if tok[0] == 'c':
                        lastc = tok
                    else:
                        dl[tok[2] % NDSEM] = tok
                if lastc:
                    final_toks.append(lastc)
                final_toks.extend(dl.values())
        block = self.es.enter_context(nc.Block())

        def run(ename, engobj):
            waited = {}

            def wait_tok(t):
                sem, val = self.tok2sem(t)
                sid = id(sem)
                if waited.get(sid, 0) < val:
                    engobj.wait_ge(sem, val)
                    waited[sid] = val
            for fn, deps, tok in self.ops[ename]:
                for d in sorted(deps):
                    if d[0] == 'c' and d[1] == ename:
                        if ename == 'pe' or not SAME_ENGINE_SYNC:
                            continue
                    wait_tok(d)
                if tok[0] == 'd' and tok[2] >= NDSEM:
                    wait_tok(('d', ename, tok[2] - NDSEM))
                inst = fn(engobj)
                sem, val = self.tok2sem(tok)
                inst.then_inc(sem, 16 if tok[0] == 'd' else 1)
            if ename == 'sp':
                for t in sorted(final_toks):
                    wait_tok(t)

        if self.ops['sp'] or True:
            @block.sync
            def _(eng):
                run('sp', eng)
        if self.ops['pe']:
            @block.tensor
            def _(eng):
                run('pe', eng)
        if self.ops['dve']:
            @block.vector
            def _(eng):
                run('dve', eng)
        if self.ops['act']:
            @block.scalar
            def _(eng):
                run('act', eng)
        if self.ops['pool']:
            @block.gpsimd
            def _(eng):
                run('pool', eng)


D = 1024
DFF = 2816
NFC = DFF // 128


def even_plan():
    fm = []
    def addfm(col, n, row, scale=1.0, rope=None):
        for i in range(n // 128):
            fm.append(dict(col=col + i * 128, row=row + i * 128, scale=scale,
                           rope=None if rope is None else rope + i * 128))
    addfm(0, 512, 0, 0.125)
    addfm(512, 512, 512)
    addfm(1536, 512, 1024, 0.125, rope=2840)
    addfm(2048, 128, 1536)
    addfm(2176, 128, 1664)
    addfm(2304, 128, 1792, 1.0, rope=2840 + 512)
    addfm(2560, 128, 1920, 1.0, rope=2840 + 640)
    tm = [dict(col=1024, n=512, ocol=0), dict(col=2432, n=128, ocol=512), dict(col=2688, n=128, ocol=640)]
    return dict(ncols=2840 + 768, fm=fm, fm_rows=2048, fm32=dict(col=2816, n=24), tm=tm, tm_cols=768)


def odd_plan():
    fm = []
    for i in range(8):
        fm.append(dict(col=i * 128, row=i * 128, scale=0.125, rope=None))
    for i in range(8):
        fm.append(dict(col=1024 + i * 128, row=1024 + i * 128, scale=1.0, rope=None))
    tm = [dict(col=2048, n=512, ocol=0), dict(col=2560, n=512, ocol=512)]
    return dict(ncols=3088, fm=fm, fm_rows=2048, fm32=dict(col=3072, n=16), tm=tm, tm_cols=1024)


def build_dense(NT, has_prev, next_kind, final, HALO=128):
    nc = bass.Bass("TRN2", target_bir_lowering=False)
    es = ExitStack()
    S = Sched(nc, es)
    din = lambda name, shape, dt: nc.dram_tensor(name, shape, dt, kind="ExternalInput").ap()
    dout = lambda name, shape, dt: nc.dram_tensor(name, shape, dt, kind="ExternalOutput").ap()
    dscr = lambda name, shape, dt: nc.dram_tensor(name, shape, dt, kind="Internal").ap()
    sb = lambda name, shape, dt: es.enter_context(nc.sbuf_tensor(name, shape, dt))
    psum = lambda name, shape, dt: es.enter_context(nc.psum_tensor(name, shape, dt))

    H = HALO if has_prev else 0
    x_in = din("x_in", [H + NT, D], F32)
    ident_d = din("ident", [128, 128], BF16)
    plan = None
    if next_kind == 'even':
        plan = even_plan()
    elif next_kind == 'odd':
        plan = odd_plan()
    if has_prev:
        oT_d = din("oT", [D, H + NT], BF16)
        wo_d = din("w_o", [D, D], F32)
        g2_d = din("g2", [128, D], F32)
        wfi_d = din("w_fi", [D, 2 * DFF], F32)
        wfo_d = din("w_fo", [DFF, D], F32)
        cw_d = din("conv_w", [128, NFC, 4], F32)
        flag_d = din("flag", [128, 1], F32)
        gT_s = dscr("gT_s", [DFF, NT], BF16)
        xmid_s = dscr("xmid_s", [NT, D], F32)
    if plan is not None:
        g1_d = din("g1", [128, D], F32)
        wi_d = din("w_i", [D, plan['ncols']], F32)
        fm_o = dout("fm", [plan['fm_rows'], NT], BF16)
        fm32_o = dout("fm32", [plan['fm32']['n'], NT], F32)
        tm_o = dout("tm", [NT, plan['tm_cols']], BF16)
        if next_kind == 'even':
            cos_d = din("cosT", [128, NT], F32)
            sin_d = din("sinT", [128, NT], F32)
    if final:
        gF_d = din("gF", [128, D], F32)
        y_o = dout("y", [NT, D], F32)
    elif has_prev or True:
        x_o = dout("x_out", [NT, D], F32) if has_prev else None

    ident = sb("ident_sb", [128, 128], BF16)
    S.add('sp', lambda e: e.dma_start(out=ident[:], in_=ident_d[:, :]), writes=['ident'], dma=True)
    SW = 1024
    stage = [sb(f"stage{i}", [128, SW], F32) for i in range(2)]
    warena = sb("warena", [128, 53248], BF16)
    cvt_i = [0]

    def load_w(w_d, wb, K, N, key):
        for kc in range(K // 128):
            for c0 in range(0, N, SW):
                cn = min(SW, N - c0)
                i = cvt_i[0]
                cvt_i[0] += 1
                st = stage[i % 2]
                sk = f"stage{i % 2}"
                S.add('sp', lambda e, st=st, kc=kc, c0=c0, cn=cn: e.dma_start(
                    out=st[:, :cn], in_=w_d[kc * 128:(kc + 1) * 128, c0:c0 + cn]), writes=[sk], dma=True)
                eng = ['act', 'pool', 'dve'][i % 3]
                if eng == 'act':
                    S.add('act', lambda e, st=st, kc=kc, c0=c0, cn=cn: e.activation(
                        out=wb[:, kc, c0:c0 + cn], in_=st[:, :cn], func=AF.Copy), reads=[sk], writes=[key])
                else:
                    S.add(eng, lambda e, st=st, kc=kc, c0=c0, cn=cn: e.tensor_copy(
                        wb[:, kc, c0:c0 + cn], st[:, :cn]), reads=[sk], writes=[key])

    xg = [sb(f"xg{i}", [128, D], F32) for i in range(2)]
    junk = sb("junk", [128, D], BF16)
    hb = sb("hb", [128, D], BF16)
    hT = sb("hT", [128, 8, 512], BF16)
    stat = sb("stat", [128, 8], F32)
    ptr = psum("ptr", [128, 1024], BF16)
    pm = [psum(f"pm{i}", [128, 512], F32) for i in range(7)]

    def norm_to_hT(xt, xk, g_sb, gk, i, cnt):
        c = cnt % 4
        ss = stat[:, 2 * c:2 * c + 1]
        rs = stat[:, 2 * c + 1:2 * c + 2]
        sk = f"stat{c}"
        S.add('act', lambda e: e.activation(out=junk[:], in_=xt[:], func=AF.Square, scale=1.0 / 32.0, accum_out=ss),
              reads=[xk], writes=['junk', sk])
        S.add('dve', lambda e: e.tensor_scalar(out=ss, in0=ss, scalar1=1e-6, scalar2=None, op0=ALU.add),
              reads=[sk], writes=[sk])
        S.add('act', lambda e: e.activation(out=ss, in_=ss, func=AF.Sqrt), reads=[sk], writes=[sk])
        S.add('dve', lambda e: e.reciprocal(out=rs, in_=ss), reads=[sk], writes=[sk + 'r'])
        S.add('dve', lambda e: e.scalar_tensor_tensor(out=hb[:], in0=xt[:], scalar=rs, in1=g_sb[:],
                                                      op0=ALU.mult, op1=ALU.mult),
              reads=[xk, sk + 'r', gk], writes=['hb'])
        for kc in range(8):
            S.add('pe', lambda e, kc=kc: e.transpose(out=ptr[:, kc * 128:(kc + 1) * 128],
                                                      in_=hb[:, kc * 128:(kc + 1) * 128], identity=ident[:]),
                  reads=['hb', 'ident'], writes=['ptr'])
        S.add('act', lambda e: e.activation(out=hT[:, :, i * 128:(i + 1) * 128],
                                            in_=ptr[:].rearrange("p (k t) -> p k t", k=8), func=AF.Copy),
              reads=['ptr'], writes=['hT'])
        return rs, sk + 'r'

    ngroups = NT // 512

    if has_prev:
        wo = warena[:, 0:8 * D].rearrange("p (k n) -> p k n", k=8)
        wfi = warena[:, 8 * D:8 * D + 16 * DFF].rearrange("p (k n) -> p k n", k=8)
        g2 = sb("g2_sb", [128, D], F32)
        cw = sb("cw", [128, NFC, 4], F32)
        flag = sb("flag_sb", [128, 1], F32)
        S.add('sp', lambda e: e.dma_start(out=g2[:], in_=g2_d[:, :]), writes=['g2'], dma=True)
        S.add('sp', lambda e: e.dma_start(out=cw[:], in_=cw_d[:, :, :]), writes=['cw'], dma=True)
        S.add('sp', lambda e: e.dma_start(out=flag[:], in_=flag_d[:, :]), writes=['flag'], dma=True)
        load_w(wo_d, wo, D, D, 'wo')
        load_w(wfi_d, wfi, D, 2 * DFF, 'wfi')
        oTg = [sb(f"oTg{i}", [128, 8, 128], BF16) for i in range(2)]
        carry = sb("carry", [128, NFC, 2], F32)
        a_sb = [sb(f"a_sb{i}", [128, 514], F32) for i in range(2)]
        t1 = [sb(f"t1_{i}", [128, 512], F32) for i in range(2)]
        sg = [sb(f"sg{i}", [128, 512], F32) for i in range(2)]
        go = [sb(f"go{i}", [128, 512], BF16) for i in range(2)]
        S.add('pool', lambda e: e.memset(carry[:], 0.0), writes=['carry'])

        groups = [(0, HALO, True)] + [(HALO + g * 512, 512, False) for g in range(ngroups)]
        cnt = 0
        jj = 0
        for (t0, n, is_halo) in groups:
            nsub = n // 128
            for i in range(nsub):
                b = cnt % 2
                xt = xg[b]
                xk = f"xg{b}"
                ot = oTg[b]
                ok = f"oTg{b}"
                tok = t0 + i * 128
                S.add('sp', lambda e, xt=xt, tok=tok: e.dma_start(out=xt[:], in_=x_in[tok:tok + 128, :]),
                      writes=[xk], dma=True)
                S.add('sp', lambda e, ot=ot, tok=tok: e.dma_start(
                    out=ot[:], in_=oT_d[:, tok:tok + 128].rearrange("(k p) t -> p k t", p=128)),
                    writes=[ok], dma=True)
                for half in range(2):
                    pp = pm[half]
                    for kc in range(8):
                        S.add('pe', lambda e, pp=pp, ot=ot, kc=kc, half=half: e.matmul(
                            pp[:, :], lhsT=ot[:, kc, :], rhs=wo[:, kc, half * 512:(half + 1) * 512],
                            start=(kc == 0), stop=(kc == 7)), reads=[ok, 'wo'], writes=[f"pm{half}"])
                    S.add('dve', lambda e, pp=pp, xt=xt, half=half: e.tensor_tensor(
                        out=xt[:, half * 512:(half + 1) * 512], in0=xt[:, half * 512:(half + 1) * 512],
                        in1=pp[:, :], op=ALU.add), reads=[xk, f"pm{half}"], writes=[xk])
                if not is_halo:
                    o0 = tok - HALO
                    S.add('sp', lambda e, xt=xt, o0=o0: e.dma_start(out=xmid_s[o0:o0 + 128, :], in_=xt[:]),
                          reads=[xk], writes=['xmid_s'], dma=True)
                norm_to_hT(xt, xk, g2, 'g2', i, cnt)
                cnt += 1
            for j in range(NFC):
                b = jj % 2
                jj += 1
                pa = pm[2 + b]
                pb = pm[4 + b]
                for kc in range(8):
                    S.add('pe', lambda e, pa=pa, kc=kc, j=j, n=n: e.matmul(
                        pa[:, :n], lhsT=wfi[:, kc, j * 128:(j + 1) * 128], rhs=hT[:, kc, :n],
                        start=(kc == 0), stop=(kc == 7)), reads=['wfi', 'hT'], writes=[f"pm{2 + b}"])
                if not is_halo:
                    for kc in range(8):
                        S.add('pe', lambda e, pb=pb, kc=kc, j=j, n=n: e.matmul(
                            pb[:, :n], lhsT=wfi[:, kc, DFF + j * 128:DFF + (j + 1) * 128], rhs=hT[:, kc, :n],
                            start=(kc == 0), stop=(kc == 7)), reads=['wfi', 'hT'], writes=[f"pm{4 + b}"])
                ab = a_sb[b]
                ak = f"a_sb{b}"
                S.add('act', lambda e, ab=ab, pa=pa, n=n: e.activation(out=ab[:, 2:2 + n], in_=pa[:, :n], func=AF.Copy),
                      reads=[f"pm{2 + b}"], writes=[ak])
                S.add('pool', lambda e, ab=ab, j=j: e.tensor_copy(ab[:, 0:2], carry[:, j, :]),
                      reads=['carry'], writes=[ak])
                if is_halo:
                    S.add('pool', lambda e, ab=ab, j=j, n=n: e.tensor_scalar(
                        out=carry[:, j, :], in0=ab[:, n:n + 2], scalar1=flag[:, 0:1], scalar2=None, op0=ALU.mult),
                        reads=[ak, 'flag'], writes=['carry'])
                    continue
                S.add('pool', lambda e, ab=ab, j=j, n=n: e.tensor_copy(carry[:, j, :], ab[:, n:n + 2]),
                      reads=[ak], writes=['carry'])
                tt = t1[b]
                tk = f"t1_{b}"
                S.add('dve', lambda e, tt=tt, ab=ab, j=j, n=n: e.tensor_scalar(
                    out=tt[:, :n], in0=ab[:, 2:2 + n], scalar1=cw[:, j, 2:3], scalar2=None, op0=ALU.mult),
                    reads=[ak, 'cw'], writes=[tk])
                S.add('dve', lambda e, tt=tt, ab=ab, j=j, n=n: e.scalar_tensor_tensor(
                    out=tt[:, :n], in0=ab[:, 1:1 + n], scalar=cw[:, j, 1:2], in1=tt[:, :n],
                    op0=ALU.mult, op1=ALU.add), reads=[ak, 'cw', tk], writes=[tk])
                S.add('dve', lambda e, tt=tt, ab=ab, j=j, n=n: e.scalar_tensor_tensor(
                    out=tt[:, :n], in0=ab[:, 0:n], scalar=cw[:, j, 0:1], in1=tt[:, :n],
                    op0=ALU.mult, op1=ALU.add), reads=[ak, 'cw', tk], writes=[tk])
                ss_ = sg[b]
                sk_ = f"sg{b}"
                S.add('act', lambda e, ss_=ss_, tt=tt, j=j, n=n: e.activation(
                    out=ss_[:, :n], in_=tt[:, :n], func=AF.Silu, bias=cw[:, j, 3:4]),
                    reads=[tk, 'cw'], writes=[sk_])
                gg = go[b]
                gk = f"go{b}"
                S.add('dve', lambda e, gg=gg, ss_=ss_, pb=pb, n=n: e.tensor_tensor(
                    out=gg[:, :n], in0=ss_[:, :n], in1=pb[:, :n], op=ALU.mult),
                    reads=[sk_, f"pm{4 + b}"], writes=[gk])
                o0 = t0 - HALO
                S.add('sp', lambda e, gg=gg, j=j, o0=o0, n=n: e.dma_start(
                    out=gT_s[j * 128:(j + 1) * 128, o0:o0 + n], in_=gg[:, :n]),
                    reads=[gk], writes=['gT_s'], dma=True)

    if has_prev:
        es_w = None
    S.barrier()
    if has_prev:
        wfo = warena[:, 0:NFC * D].rearrange("p (k n) -> p k n", k=NFC)
        load_w(wfo_d, wfo, DFF, D, 'wfo')
        gTg = [sb(f"gTg{i}", [128, NFC, 128], BF16) for i in range(2)]
    if plan is not None:
        wi = warena[:, NFC * D:NFC * D + 8 * plan['ncols']].rearrange("p (k n) -> p k n", k=8)
        g1 = sb("g1_sb", [128, D], F32)
        S.add('sp', lambda e: e.dma_start(out=g1[:], in_=g1_d[:, :]), writes=['g1'], dma=True)
        load_w(wi_d, wi, D, plan['ncols'], 'wi')
        fmo = [sb(f"fmo{i}", [128, 512], BF16) for i in range(2)]
        fm32s = sb("fm32s", [32, 512], F32)
        tmo = [sb(f"tmo{i}", [128, 512], BF16) for i in range(2)]
        if next_kind == 'even':
            cosb = [sb(f"cosT_sb{i}", [128, 512], F32) for i in range(2)]
            sinb = [sb(f"sinT_sb{i}", [128, 512], F32) for i in range(2)]
            r1 = [sb(f"r1_{i}", [128, 512], F32) for i in range(2)]
            r2 = [sb(f"r2_{i}", [128, 512], F32) for i in range(2)]
    if final:
        gF = sb("gF_sb", [128, D], F32)
        S.add('sp', lambda e: e.dma_start(out=gF[:], in_=gF_d[:, :]), writes=['gF'], dma=True)
        yb = [sb(f"yb{i}", [128, D], F32) for i in range(2)]

    cnt = 0
    ev = 0
    for g in range(ngroups):
        for i in range(4):
            b = cnt % 2
            xt = xg[b]
            xk = f"xg{b}"
            tok = g * 512 + i * 128
            if has_prev:
                gt = gTg[b]
                gk = f"gTg{b}"
                S.add('sp', lambda e, xt=xt, tok=tok: e.dma_start(out=xt[:], in_=xmid_s[tok:tok + 128, :]),
                      reads=['xmid_s'], writes=[xk], dma=True)
                S.add('sp', lambda e, gt=gt, tok=tok: e.dma_start(
                    out=gt[:], in_=gT_s[:, tok:tok + 128].rearrange("(j p) t -> p j t", p=128)),
                    reads=['gT_s'], writes=[gk], dma=True)
                for half in range(2):
                    pp = pm[half]
                    for j in range(NFC):
                        S.add('pe', lambda e, pp=pp, gt=gt, j=j, half=half: e.matmul(
                            pp[:, :], lhsT=gt[:, j, :], rhs=wfo[:, j, half * 512:(half + 1) * 512],
                            start=(j == 0), stop=(j == NFC - 1)), reads=[gk, 'wfo'], writes=[f"pm{half}"])
                    S.add('dve', lambda e, pp=pp, xt=xt, half=half: e.tensor_tensor(
                        out=xt[:, half * 512:(half + 1) * 512], in0=xt[:, half * 512:(half + 1) * 512],
                        in1=pp[:, :], op=ALU.add), reads=[xk, f"pm{half}"], writes=[xk])
                if not final:
                    S.add('sp', lambda e, xt=xt, tok=tok: e.dma_start(out=x_o[tok:tok + 128, :], in_=xt[:]),
                          reads=[xk], dma=True)
            else:
                S.add('sp', lambda e, xt=xt, tok=tok: e.dma_start(out=xt[:], in_=x_in[tok:tok + 128, :]),
                      writes=[xk], dma=True)
            if final:
                c = cnt % 4
                ss = stat[:, 2 * c:2 * c + 1]
                rs = stat[:, 2 * c + 1:2 * c + 2]
                sk = f"stat{c}"
                S.add('act', lambda e, xt=xt, ss=ss: e.activation(out=junk[:], in_=xt[:], func=AF.Square, scale=1.0 / 32.0, accum_out=ss),
                      reads=[xk], writes=['junk', sk])
                S.add('dve', lambda e, ss=ss: e.tensor_scalar(out=ss, in0=ss, scalar1=1e-6, scalar2=None, op0=ALU.add),
                      reads=[sk], writes=[sk])
                S.add('act', lambda e, ss=ss: e.activation(out=ss, in_=ss, func=AF.Sqrt), reads=[sk], writes=[sk])
                S.add('dve', lambda e, ss=ss, rs=rs: e.reciprocal(out=rs, in_=ss), reads=[sk], writes=[sk + 'r'])
                yy = yb[b]
                yk = f"yb{b}"
                S.add('dve', lambda e, yy=yy, xt=xt, rs=rs: e.scalar_tensor_tensor(
                    out=yy[:], in0=xt[:], scalar=rs, in1=gF[:], op0=ALU.mult, op1=ALU.mult),
                    reads=[xk, sk + 'r', 'gF'], writes=[yk])
                S.add('sp', lambda e, yy=yy, tok=tok: e.dma_start(out=y_o[tok:tok + 128, :], in_=yy[:]),
                      reads=[yk], dma=True)
            if plan is not None:
                norm_to_hT(xt, xk, g1, 'g1', i, cnt)
            cnt += 1
        if plan is None:
            continue
        T0 = g * 512
        if next_kind == 'even':
            cosT = cosb[g % 2]
            sinT = sinb[g % 2]
            ck = f"cos{g % 2}"
            S.add('sp', lambda e, cosT=cosT, T0=T0: e.dma_start(out=cosT[:], in_=cos_d[:, T0:T0 + 512]), writes=[ck], dma=True)
            S.add('sp', lambda e, sinT=sinT, T0=T0: e.dma_start(out=sinT[:], in_=sin_d[:, T0:T0 + 512]), writes=[ck + 's'], dma=True)
        for ch in plan['fm']:
            b = ev % 2
            ev += 1
            pa = pm[2 + b]
            for kc in range(8):
                S.add('pe', lambda e, pa=pa, kc=kc, ch=ch: e.matmul(
                    pa[:, :], lhsT=wi[:, kc, ch['col']:ch['col'] + 128], rhs=hT[:, kc, :],
                    start=(kc == 0), stop=(kc == 7)), reads=['wi', 'hT'], writes=[f"pm{2 + b}"])
            fo = fmo[b]
            fk = f"fmo{b}"
            if ch['rope'] is None:
                S.add('act', lambda e, fo=fo, pa=pa, ch=ch: e.activation(out=fo[:], in_=pa[:, :], func=AF.Copy,
                                                                         scale=float(ch['scale'])),
                      reads=[f"pm{2 + b}"], writes=[fk])
            else:
                pb = pm[4 + b]
                for kc in range(8):
                    S.add('pe', lambda e, pb=pb, kc=kc, ch=ch: e.matmul(
                        pb[:, :], lhsT=wi[:, kc, ch['rope']:ch['rope'] + 128], rhs=hT[:, kc, :],
                        start=(kc == 0), stop=(kc == 7)), reads=['wi', 'hT'], writes=[f"pm{4 + b}"])
                ra = r1[b]
                rb = r2[b]
                S.add('dve', lambda e, ra=ra, pa=pa, ch=ch, cosT=cosT: e.scalar_tensor_tensor(
                    out=ra[:], in0=pa[:, :], scalar=float(ch['scale']), in1=cosT[:],
                    op0=ALU.mult, op1=ALU.mult), reads=[f"pm{2 + b}", ck], writes=[f"r1_{b}"])
                S.add('dve', lambda e, rb=rb, pb=pb, ch=ch, sinT=sinT: e.scalar_tensor_tensor(
                    out=rb[:], in0=pb[:, :], scalar=float(ch['scale']), in1=sinT[:],
                    op0=ALU.mult, op1=ALU.mult), reads=[f"pm{4 + b}", ck + 's'], writes=[f"r2_{b}"])
                S.add('pool', lambda e, fo=fo, ra=ra, rb=rb: e.tensor_tensor(out=fo[:], in0=ra[:], in1=rb[:], op=ALU.add),
                      reads=[f"r1_{b}", f"r2_{b}"], writes=[fk])
            S.add('sp', lambda e, fo=fo, ch=ch, T0=T0: e.dma_start(
                out=fm_o[ch['row']:ch['row'] + 128, T0:T0 + 512], in_=fo[:]), reads=[fk], dma=True)
        f32 = plan['fm32']
        b = ev % 2
        ev += 1
        pa = pm[2 + b]
        for kc in range(8):
            S.add('pe', lambda e, pa=pa, kc=kc: e.matmul(
                pa[:f32['n'], :], lhsT=wi[:, kc, f32['col']:f32['col'] + f32['n']], rhs=hT[:, kc, :],
                start=(kc == 0), stop=(kc == 7)), reads=['wi', 'hT'], writes=[f"pm{2 + b}"])
        S.add('act', lambda e, pa=pa: e.activation(out=fm32s[:f32['n'], :], in_=pa[:f32['n'], :], func=AF.Copy),
              reads=[f"pm{2 + b}"], writes=['fm32s'])
        S.add('sp', lambda e, T0=T0: e.dma_start(out=fm32_o[:, T0:T0 + 512], in_=fm32s[:f32['n'], :]),
              reads=['fm32s'], dma=True)
        for i in range(4):
            for tc in plan['tm']:
                b = ev % 2
                ev += 1
                pa = pm[2 + b]
                n = tc['n']
                for kc in range(8):
                    S.add('pe', lambda e, pa=pa, kc=kc, tc=tc, i=i, n=n: e.matmul(
                        pa[:, :n], lhsT=hT[:, kc, i * 128:(i + 1) * 128], rhs=wi[:, kc, tc['col']:tc['col'] + n],
                        start=(kc == 0), stop=(kc == 7)), reads=['wi', 'hT'], writes=[f"pm{2 + b}"])
                to = tmo[b]
                tk = f"tmo{b}"
                S.add('act', lambda e, to=to, pa=pa, n=n: e.activation(out=to[:, :n], in_=pa[:, :n], func=AF.Copy),
                      reads=[f"pm{2 + b}"], writes=[tk])
                tok = T0 + i * 128
                S.add('sp', lambda e, to=to, tc=tc, tok=tok, n=n: e.dma_start(
                    out=tm_o[tok:tok + 128, tc['ocol']:tc['ocol'] + n], in_=to[:, :n]), reads=[tk], dma=True)
    S.emit()
    es.close()
    return nc


def build_fox(S_, NH=8):
    nc = bass.Bass("TRN2", target_bir_lowering=False)
    es = ExitStack()
    S = Sched(nc, es)
    din = lambda name, shape, dt: nc.dram_tensor(name, shape, dt, kind="ExternalInput").ap()
    dout = lambda name, shape, dt: nc.dram_tensor(name, shape, dt, kind="ExternalOutput").ap()
    dscr = lambda name, shape, dt: nc.dram_tensor(name, shape, dt, kind="Internal").ap()
    sb = lambda name, shape, dt: es.enter_context(nc.sbuf_tensor(name, shape, dt))
    psum = lambda name, shape, dt: es.enter_context(nc.psum_tensor(name, shape, dt))
    NTL = S_ // 128
    NG = S_ // 512

    qT_d = din("qT", [NH, 64, S_], BF16)
    kT_d = din("kT", [NH, 64, S_], BF16)
    v_d = din("vaug", [NH, 128, NTL, 65], BF16)
    fT_d = din("fT", [NH, S_], F32)
    negb_d = din("bf", [NH, 1], F32)
    tri_d = din("tri", [128, 128], BF16)
    sel_d = din("sel64", [65, 64], F32)
    oT_o = dout("oT", [NH * 64, S_], BF16)
    cs_s = dscr("cs_s", [NH, 6, S_], BF16)

    tri = sb("tri_sb", [128, 128], BF16)
    sel64 = sb("sel_sb", [65, 64], F32)
    negb = sb("negb_sb", [NH, 1], F32)
    S.add('sp', lambda e: e.dma_start(out=tri[:], in_=tri_d[:, :]), writes=['tri'], dma=True)
    S.add('sp', lambda e: e.dma_start(out=sel64[:], in_=sel_d[:, :]), writes=['sel64'], dma=True)
    S.add('sp', lambda e: e.dma_start(out=negb[:], in_=negb_d[:, :]), writes=['negb'], dma=True)
    S.add('dve', lambda e: e.tensor_scalar(out=negb[:], in0=negb[:], scalar1=-1.0, scalar2=None, op0=ALU.mult),
          reads=['negb'], writes=['negb'])

    fw_ = sb("fw", [NH, S_], F32)
    ones = sb("ones", [NH, S_], BF16)
    cs = sb("cs", [NH, S_], F32)
    CH = min(2048, S_)
    sp6 = sb("sp6", [NH, 6, CH], BF16)
    S.add('sp', lambda e: e.dma_start(out=fw_[:], in_=fT_d[:, :]), writes=['fw'], dma=True)
    S.add('pool', lambda e: e.memset(ones[:], 1.0), writes=['ones'])
    S.add('act', lambda e: e.activation(out=fw_[:], in_=fw_[:], func=AF.Exp, scale=-1.0, bias=negb[:, 0:1]),
          reads=['fw', 'negb'], writes=['fw'])
    S.add('dve', lambda e: e.tensor_scalar(out=fw_[:], in0=fw_[:], scalar1=1.0, scalar2=None, op0=ALU.add),
          reads=['fw'], writes=['fw'])
    S.add('act', lambda e: e.activation(out=fw_[:], in_=fw_[:], func=AF.Ln), reads=['fw'], writes=['fw'])
    S.add('dve', lambda e: e.tensor_tensor_scan(out=cs[:], data0=ones[:], data1=fw_[:], initial=0.0,
                                                op0=ALU.mult, op1=ALU.add), reads=['ones', 'fw'], writes=['cs'])
    for c0 in range(0, S_, CH):
        csl = cs[:, c0:c0 + CH]
        S.add('dve', lambda e, csl=csl: e.tensor_copy(sp6[:, 0, :], csl), reads=['cs'], writes=['sp6'])
        S.add('dve', lambda e, csl=csl: e.tensor_tensor(out=csl, in0=csl, in1=sp6[:, 0, :], op=ALU.subtract),
              reads=['cs', 'sp6'], writes=['cs'])
        S.add('dve', lambda e, csl=csl: e.tensor_copy(sp6[:, 1, :], csl), reads=['cs'], writes=['sp6'])
        S.add('dve', lambda e, csl=csl: e.tensor_tensor(out=csl, in0=csl, in1=sp6[:, 1, :], op=ALU.subtract),
              reads=['cs', 'sp6'], writes=['cs'])
        S.add('dve', lambda e, csl=csl: e.tensor_copy(sp6[:, 2, :], csl), reads=['cs'], writes=['sp6'])
        for j in range(3):
            S.add('dve', lambda e, j=j: e.tensor_scalar(out=sp6[:, 3 + j, :], in0=sp6[:, j, :], scalar1=-1.0, scalar2=None,
                                                        op0=ALU.mult), reads=['sp6'], writes=['sp6'])
        S.add('sp', lambda e, c0=c0: e.dma_start(out=cs_s[:, :, c0:c0 + CH], in_=sp6[:]), reads=['sp6'], writes=['cs_s'], dma=True)

    Ka = [sb(f"Ka{i}", [70, S_], BF16) for i in range(2)]
    Qa = [sb(f"Qa{i}", [70, S_], BF16) for i in range(2)]
    Vh = [sb(f"Vh{i}", [128, NTL, 65], BF16) for i in range(2)]
    NPB = 3
    s_ps = [psum(f"s_ps{i}", [128, 512], F32) for i in range(NPB)]
    o_ps = [psum(f"o_ps{i}", [128, 512], F32) for i in range(2)]
    bc_ps = psum("bc_ps", [128, 512], F32)
    p_sb = [sb(f"p_sb{i}", [128, 512], BF16) for i in range(NPB)]
    o_sb = [sb(f"o_sb{i}", [65, 512], F32) for i in range(2)]
    rec = [sb(f"rec{i}", [64, 512], F32) for i in range(2)]
    on = [sb(f"on{i}", [64, 512], BF16) for i in range(2)]

    steps = []
    for h in range(NH):
        for G in range(NG):
            for j in range(4 * G + 4):
                steps.append((h, G, j))

    def load_head(h):
        b = h % 2
        S.add('pool', lambda e: e.memset(Ka[b][64:70, :], 1.0), writes=[f"Ka{b}"])
        S.add('pool', lambda e: e.memset(Qa[b][64:70, :], 1.0), writes=[f"Qa{b}"])
        S.add('sp', lambda e: e.dma_start(out=Ka[b][0:64, :], in_=kT_d[h, :, :]), writes=[f"Ka{b}"], dma=True)
        S.add('sp', lambda e: e.dma_start(out=Qa[b][0:64, :], in_=qT_d[h, :, :]), writes=[f"Qa{b}"], dma=True)
        S.add('sp', lambda e: e.dma_start(out=Ka[b][67:70, :], in_=cs_s[h, 0:3, :]), reads=['cs_s'], writes=[f"Ka{b}"], dma=True)
        S.add('sp', lambda e: e.dma_start(out=Qa[b][64:67, :], in_=cs_s[h, 3:6, :]), reads=['cs_s'], writes=[f"Qa{b}"], dma=True)
        S.add('sp', lambda e: e.dma_start(out=Vh[b][:], in_=v_d[h, :, :, :]), writes=[f"Vh{b}"], dma=True)

    def qk(t):
        h, G, j = steps[t]
        b = h % 2
        r = max(0, j - 4 * G)
        c0 = r * 128
        sp_ = s_ps[t % NPB]
        S.add('pe', lambda e: e.matmul(sp_[:, c0:512], lhsT=Ka[b][0:70, j * 128:(j + 1) * 128],
                                       rhs=Qa[b][0:70, G * 512 + c0:(G + 1) * 512], start=True, stop=True),
              reads=[f"Ka{b}", f"Qa{b}"], writes=[f"s_ps{t % NPB}"])

    def expo(t):
        h, G, j = steps[t]
        r = max(0, j - 4 * G)
        c0 = r * 128
        sp_ = s_ps[t % NPB]
        pb = p_sb[t % NPB]
        S.add('act', lambda e: e.activation(out=pb[:, c0:512], in_=sp_[:, c0:512], func=AF.Exp),
              reads=[f"s_ps{t % NPB}"], writes=[f"p_sb{t % NPB}"])
        if j >= 4 * G:
            S.add('pool', lambda e: e.tensor_tensor(out=pb[:, c0:c0 + 128], in0=pb[:, c0:c0 + 128], in1=tri[:], op=ALU.mult),
                  reads=[f"p_sb{t % NPB}", 'tri'], writes=[f"p_sb{t % NPB}"])

    gcount = [0]

    def pv(t):
        h, G, j = steps[t]
        b = h % 2
        r = max(0, j - 4 * G)
        c0 = r * 128
        pb = p_sb[t % NPB]
        gi = (h * NG + G) % 2
        op_ = o_ps[gi]
        last = (j == 4 * G + 3)
        S.add('pe', lambda e: e.matmul(op_[0:65, c0:512], lhsT=Vh[b][:, j, :], rhs=pb[:, c0:512],
                                       start=(j == 0), stop=last),
              reads=[f"Vh{b}", f"p_sb{t % NPB}"], writes=[f"o_ps{gi}"])
        if last:
            ob = o_sb[gi]
            S.add('act', lambda e: e.activation(out=ob[:], in_=op_[0:65, :], func=AF.Copy),
                  reads=[f"o_ps{gi}"], writes=[f"o_sb{gi}"])
            S.add('pe', lambda e: e.matmul(bc_ps[0:64, :], lhsT=sel64[:], rhs=ob[:], start=True, stop=True),
                  reads=['sel64', f"o_sb{gi}"], writes=['bc_ps'])
            rc = rec[gi]
            S.add('dve', lambda e: e.reciprocal(out=rc[:], in_=bc_ps[0:64, :]), reads=['bc_ps'], writes=[f"rec{gi}"])
            oo = on[gi]
            S.add('dve', lambda e: e.tensor_tensor(out=oo[:], in0=ob[0:64, :], in1=rc[:], op=ALU.mult),
                  reads=[f"o_sb{gi}", f"rec{gi}"], writes=[f"on{gi}"])
            S.add('sp', lambda e: e.dma_start(out=oT_o[h * 64:(h + 1) * 64, G * 512:(G + 1) * 512], in_=oo[:]),
                  reads=[f"on{gi}"], dma=True)

    load_head(0)
    nst = len(steps)
    for t in range(nst):
        h, G, j = steps[t]
        if G == 0 and j == 0 and h + 1 < NH:
            load_head(h + 1)
        if t == 0:
            qk(0)
        expo(t)
        if t + 1 < nst:
            qk(t + 1)
        pv(t)
    S.emit()
    es.close()
    return nc


def build_sb(S_, NH=4):
    nc = bass.Bass("TRN2", target_bir_lowering=False)
    es = ExitStack()
    S = Sched(nc, es)
    din = lambda name, shape, dt: nc.dram_tensor(name, shape, dt, kind="ExternalInput").ap()
    dout = lambda name, shape, dt: nc.dram_tensor(name, shape, dt, kind="ExternalOutput").ap()
    sb = lambda name, shape, dt: es.enter_context(nc.sbuf_tensor(name, shape, dt))
    psum = lambda name, shape, dt: es.enter_context(nc.psum_tensor(name, shape, dt))
    NTL = S_ // 128
    NG = S_ // 512
    NP = NH // 2

    qT_d = din("qT", [NP, 128, S_], BF16)
    kT_d = din("kT", [NP, 128, S_], BF16)
    v_d = din("v", [NH, 128, NTL, 64], BF16)
    tris_d = din("tris", [128, 128], BF16)
    negIT_d = din("negIT", [128, 128], BF16)
    negOnes_d = din("negOnes", [128, 128], BF16)
    zeros_d = din("zeros", [128, 64], BF16)
    oT_o = dout("oT", [NH * 64, S_], BF16)

    def const(name, d, shape):
        t = sb(name, shape, BF16)
        S.add('sp', lambda e: e.dma_start(out=t[:], in_=d), writes=[name], dma=True)
        return t
    tris = const("tris_sb", tris_d[:, :], [128, 128])
    negIT = const("negIT_sb", negIT_d[:, :], [128, 128])
    negOnes = const("negOnes_sb", negOnes_d[:, :], [128, 128])
    zeros = const("zeros_sb", zeros_d[:, :], [128, 64])

    qT = [sb(f"qT{i}", [128, S_], BF16) for i in range(NP)]
    kT = [sb(f"kT{i}", [128, S_], BF16) for i in range(NP)]
    V = [sb(f"V{i}", [128, NTL, 64], BF16) for i in range(NH)]
    for i in range(NP):
        S.add('sp', lambda e, i=i: e.dma_start(out=qT[i][:], in_=qT_d[i, :, :]), writes=[f"qT{i}"], dma=True)
        S.add('sp', lambda e, i=i: e.dma_start(out=kT[i][:], in_=kT_d[i, :, :]), writes=[f"kT{i}"], dma=True)
    for h in range(NH):
        S.add('sp', lambda e, h=h: e.dma_start(out=V[h][:], in_=v_d[h, :, :, :]), writes=[f"V{h}"], dma=True)

    z_ps = [psum(f"z_ps{s}", [128, 512], F32) for s in range(2)]
    r_ps = [psum(f"r_ps{s}", [128, 512], F32) for s in range(2)]
    o_ps = [psum(f"o_ps{s}", [128, 512], F32) for s in range(2)]
    e_sb = [sb(f"e_sb{s}", [128, 512], F32) for s in range(2)]
    lp = [sb(f"lp{s}", [128, 512], BF16) for s in range(2)]
    t_sb = [sb(f"t_sb{s}", [128, 512], F32) for s in range(2)]
    a_sb = [sb(f"a_sb{s}", [128, 512], BF16) for s in range(2)]
    R = [[sb(f"R{s}_{i}", [128, 512], F32) for i in range(2)] for s in range(2)]
    oo = [sb(f"oo{s}", [64, 512], BF16) for s in range(2)]

    def do_pair(pr):
        for G in range(NG):
            nst = 4 * G + 4
            for s in range(2):
                S.add('pe', lambda e, s=s, G=G: e.matmul(o_ps[s][0:64, :], lhsT=zeros[:, :], rhs=qT[pr][:, G * 512:(G + 1) * 512],
                                                         start=True, stop=False),
                      reads=['zeros_sb', f"qT{pr}"], writes=[f"o_ps{s}"])
                S.add('pool', lambda e, s=s: e.memset(R[s][0][:], 0.0), writes=[f"R{s}_0"])
                S.add('pool', lambda e, s=s: e.memset(R[s][1][:], 0.0), writes=[f"R{s}_1"])
            for st in range(nst):
                j = nst - 1 - st
                r = max(0, j - 4 * G)
                c0 = r * 128
                diag = j >= 4 * G
                cur = st % 2
                nxt = 1 - cur
                last = (st == nst - 1)
                qs = slice(G * 512 + c0, (G + 1) * 512)
                ks = slice(j * 128, (j + 1) * 128)
                cs_ = slice(c0, 512)
                ds_ = slice(c0, c0 + 128)
                for s in range(2):
                    ps_ = slice(64 * s, 64 * s + 64)
                    S.add('pe', lambda e, s=s, ps_=ps_, ks=ks, qs=qs, cs_=cs_: e.matmul(
                        z_ps[s][:, cs_], lhsT=kT[pr][ps_, ks], rhs=qT[pr][ps_, qs], start=True, stop=False),
                        reads=[f"kT{pr}", f"qT{pr}"], writes=[f"z_ps{s}"])
                for s in range(2):
                    S.add('act', lambda e, s=s, cs_=cs_: e.activation(out=e_sb[s][:, cs_], in_=z_ps[s][:, cs_], func=AF.Exp),
                          reads=[f"z_ps{s}"], writes=[f"e_sb{s}"])
                for s in range(2):
                    S.add('dve', lambda e, s=s, cs_=cs_: e.tensor_scalar(out=e_sb[s][:, cs_], in0=e_sb[s][:, cs_], scalar1=1.0,
                                                                         scalar2=None, op0=ALU.add),
                          reads=[f"e_sb{s}"], writes=[f"e_sb{s}"])
                for s in range(2):
                    S.add('act', lambda e, s=s, cs_=cs_: e.activation(out=lp[s][:, cs_], in_=e_sb[s][:, cs_], func=AF.Ln),
                          reads=[f"e_sb{s}"], writes=[f"lp{s}"])
                    if diag:
                        S.add('pool', lambda e, s=s, ds_=ds_: e.tensor_tensor(out=lp[s][:, ds_], in0=lp[s][:, ds_], in1=tris[:], op=ALU.mult),
                              reads=[f"lp{s}", 'tris_sb'], writes=[f"lp{s}"])
                for s in range(2):
                    S.add('pe', lambda e, s=s, cs_=cs_: e.matmul(z_ps[s][:, cs_], lhsT=negIT[:, :], rhs=lp[s][:, cs_],
                                                                 start=False, stop=True),
                          reads=['negIT_sb', f"lp{s}"], writes=[f"z_ps{s}"])
                    if not last:
                        S.add('pe', lambda e, s=s, cs_=cs_: e.matmul(r_ps[s][:, cs_], lhsT=negOnes[:, :], rhs=lp[s][:, cs_],
                                                                     start=True, stop=True),
                              reads=['negOnes_sb', f"lp{s}"], writes=[f"r_ps{s}"])
                for s in range(2):
                    S.add('dve', lambda e, s=s, cs_=cs_, cur=cur: e.tensor_tensor(
                        out=t_sb[s][:, cs_], in0=z_ps[s][:, cs_], in1=R[s][cur][:, cs_], op=ALU.add),
                        reads=[f"z_ps{s}", f"R{s}_{cur}"], writes=[f"t_sb{s}"])
                    if not last:
                        S.add('dve', lambda e, s=s, cs_=cs_, cur=cur, nxt=nxt: e.tensor_tensor(
                            out=R[s][nxt][:, cs_], in0=r_ps[s][:, cs_], in1=R[s][cur][:, cs_], op=ALU.add),
                            reads=[f"r_ps{s}", f"R{s}_{cur}"], writes=[f"R{s}_{nxt}"])
                for s in range(2):
                    S.add('act', lambda e, s=s, cs_=cs_: e.activation(out=a_sb[s][:, cs_], in_=t_sb[s][:, cs_], func=AF.Exp),
                          reads=[f"t_sb{s}"], writes=[f"a_sb{s}"])
                    if diag:
                        S.add('pool', lambda e, s=s, ds_=ds_: e.tensor_tensor(out=a_sb[s][:, ds_], in0=a_sb[s][:, ds_], in1=tris[:], op=ALU.mult),
                              reads=[f"a_sb{s}", 'tris_sb'], writes=[f"a_sb{s}"])
                for s in range(2):
                    h = 2 * pr + s
                    S.add('pe', lambda e, s=s, h=h, j=j, cs_=cs_, last=last: e.matmul(
                        o_ps[s][0:64, cs_], lhsT=V[h][:, j, :], rhs=a_sb[s][:, cs_], start=False, stop=last),
                        reads=[f"V{h}", f"a_sb{s}"], writes=[f"o_ps{s}"])
            for s in range(2):
                h = 2 * pr + s
                S.add('act', lambda e, s=s: e.activation(out=oo[s][:], in_=o_ps[s][0:64, :], func=AF.Copy),
                      reads=[f"o_ps{s}"], writes=[f"oo{s}"])
                S.add('sp', lambda e, s=s, h=h, G=G: e.dma_start(out=oT_o[h * 64:(h + 1) * 64, G * 512:(G + 1) * 512], in_=oo[s][:]),
                      reads=[f"oo{s}"], dma=True)
    for pr in range(NP):
        do_pair(pr)
    S.emit()
    es.close()
    return nc


def build_nsa(S_):
    nc = bass.Bass("TRN2", target_bir_lowering=False)
    es = ExitStack()
    S = Sched(nc, es)
    din = lambda name, shape, dt: nc.dram_tensor(name, shape, dt, kind="ExternalInput").ap()
    dout = lambda name, shape, dt: nc.dram_tensor(name, shape, dt, kind="ExternalOutput").ap()
    sb = lambda name, shape, dt: es.enter_context(nc.sbuf_tensor(name, shape, dt))
    psum = lambda name, shape, dt: es.enter_context(nc.psum_tensor(name, shape, dt))
    NTL = S_ // 128
    NG = S_ // 512
    NC = S_ // 16 - 1
    NS = S_ // 64
    NCC = (NC + 127) // 128
    assert NS <= 128 and NC < 512

    nqT_d = din("nqT", [2, 128, S_], BF16)
    kv_d = din("kvcmpT", [128, S_], BF16)
    kslc_d = din("kslcT2", [128, S_], BF16)
    kwin_d = din("kwinT2", [128, S_], BF16)
    vslc_d = din("vslc", [128, NTL, 65], BF16)
    vwin_d = din("vwin", [128, NTL, 65], BF16)
    gates_d = din("gatesT", [12, S_], F32)
    wkv_d = din("wkv", [128, 32 * 64], F32)
    wksw_d = din("wksw", [64, 32 * 64], F32)
    pos_d = din("pos", [128, 32], F32)
    cosC_d = din("cosC", [64, 512], F32)
    sinC_d = din("sinC", [64, 512], F32)
    M_d = din("M", [128, 4, 128], BF16)
    mC_d = din("mC", [512, S_], BF16)
    bsel_d = din("bsel", [NTL, 128, 128], F32)
    F_d = din("F", [128, S_], BF16)
    tri_d = din("tri", [128, 128], BF16)
    wm_d = din("wm", [128, 8, 512], BF16)
    sel128_d = din("sel128", [65, 128], F32)
    gsel_d = din("gsel", [12, 12 * 64], F32)
    id32_d = din("id32", [128, 128], F32)
    idbf_d = din("idbf", [128, 128], BF16)
    oT_o = dout("oT", [256, S_], BF16)

    def load(name, shape, dt, src):
        t = sb(name, shape, dt)
        S.add('sp', lambda e: e.dma_start(out=t[:], in_=src), writes=[name], dma=True)
        return t
    nqT = [load(f"nqT{i}", [128, S_], BF16, nqT_d[i, :, :]) for i in range(2)]
    kv = load("kv", [128, S_], BF16, kv_d[:, :])
    kslc = load("kslc", [128, S_], BF16, kslc_d[:, :])
    kwin = load("kwin", [128, S_], BF16, kwin_d[:, :])
    vslc = load("vslc_sb", [128, NTL, 65], BF16, vslc_d[:, :, :])
    vwin = load("vwin_sb", [128, NTL, 65], BF16, vwin_d[:, :, :])
    pos = load("pos_sb", [128, 32], F32, pos_d[:, :])
    cosC = load("cosC_sb", [64, 512], F32, cosC_d[:, :])
    sinC = load("sinC_sb", [64, 512], F32, sinC_d[:, :])
    Mt = load("M_sb", [128, 4, 128], BF16, M_d[:, :, :])
    Ft = load("F_sb", [128, S_], BF16, F_d[:, :])
    tri = load("tri_sb", [128, 128], BF16, tri_d[:, :])
    wm = load("wm_sb", [128, 8, 512], BF16, wm_d[:, :, :])
    sel128 = load("sel128_sb", [65, 128], F32, sel128_d[:, :])
    gsel = load("gsel_sb", [12, 12 * 64], F32, gsel_d[:, :])
    id32 = load("id32_sb", [128, 128], F32, id32_d[:, :])
    idbf = load("idbf_sb", [128, 128], BF16, idbf_d[:, :])

    pb = [psum(f"pb{i}", [128, 512], F32) for i in range(7)]
    pbf = psum("pbf", [128, 1024], BF16)
    PB = lambda i: f"pb{i}"

    stage = sb("stage", [128, 2048], F32)
    wkv = sb("wkv_sb", [128, 32, 64], BF16)
    wksw = sb("wksw_sb", [64, 32, 64], BF16)
    S.add('sp', lambda e: e.dma_start(out=stage[:], in_=wkv_d[:, :]), writes=['stage'], dma=True)
    S.add('act', lambda e: e.activation(out=wkv[:].rearrange("p l d -> p (l d)"), in_=stage[:], func=AF.Copy),
          reads=['stage'], writes=['wkv'])
    S.add('sp', lambda e: e.dma_start(out=stage[0:64, :], in_=wksw_d[:, :]), writes=['stage'], dma=True)
    S.add('act', lambda e: e.activation(out=wksw[:].rearrange("p l d -> p (l d)"), in_=stage[0:64, :], func=AF.Copy),
          reads=['stage'], writes=['wksw'])
    kcT = sb("kcT", [128, 512], BF16)
    vc = sb("vc", [128, 4, 65], BF16)
    S.add('pool', lambda e: e.memset(kcT[:], 0.0), writes=['kcT'])
    S.add('pool', lambda e: e.memset(vc[:], 1.0), writes=['vc'])
    xl = [sb(f"xl{i}", [128, 512], BF16) for i in range(2)]
    for l in range(32):
        x = xl[l % 2]
        xk = f"xl{l % 2}"
        S.add('dve', lambda e, x=x, l=l: e.tensor_scalar(out=x[:, 0:NC], in0=kv[:, l:l + 16 * (NC - 1) + 1:16],
                                                         scalar1=pos[:, l:l + 1], scalar2=None, op0=ALU.add),
              reads=['kv', 'pos_sb'], writes=[xk])
        S.add('pe', lambda e, x=x, l=l: e.matmul(pb[0][0:64, 0:NC], lhsT=wkv[0:64, l, :], rhs=x[0:64, 0:NC],
                                                 start=(l == 0), stop=(l == 31)), reads=['wkv', xk], writes=[PB(0)])
        S.add('pe', lambda e, x=x, l=l: e.matmul(pb[1][0:64, 0:NC], lhsT=wksw[0:64, l, :], rhs=x[0:64, 0:NC],
                                                 start=(l == 0), stop=(l == 31)), reads=['wksw', xk], writes=[PB(1)])
        for cc in range(NCC):
            cn = min(128, NC - cc * 128)
            S.add('pe', lambda e, x=x, l=l, cc=cc, cn=cn: e.matmul(
                pb[2 + cc][0:cn, 0:64], lhsT=x[64:128, cc * 128:cc * 128 + cn], rhs=wkv[64:128, l, :],
                start=(l == 0), stop=(l == 31)), reads=['wkv', xk], writes=[PB(2 + cc)])
    ra = sb("ra", [64, 512], F32)
    rb = sb("rb", [64, 512], F32)
    S.add('dve', lambda e: e.tensor_tensor(out=ra[:, 0:NC], in0=pb[0][0:64, 0:NC], in1=cosC[:, 0:NC], op=ALU.mult),
          reads=[PB(0), 'cosC_sb'], writes=['ra'])
    S.add('dve', lambda e: e.tensor_tensor(out=rb[:, 0:NC], in0=pb[1][0:64, 0:NC], in1=sinC[:, 0:NC], op=ALU.mult),
          reads=[PB(1), 'sinC_sb'], writes=['rb'])
    S.add('dve', lambda e: e.tensor_tensor(out=kcT[0:64, 0:NC], in0=ra[:, 0:NC], in1=rb[:, 0:NC], op=ALU.add),
          reads=['ra', 'rb'], writes=['kcT'])
    S.add('sp', lambda e: e.dma_start(out=kcT[64:128, :], in_=kcT[0:64, :]), reads=['kcT'], writes=['kcT'], dma=True)
    for cc in range(NCC):
        cn = min(128, NC - cc * 128)
        S.add('act', lambda e, cc=cc, cn=cn: e.activation(out=vc[0:cn, cc, 0:64], in_=pb[2 + cc][0:cn, 0:64], func=AF.Copy),
              reads=[PB(2 + cc)], writes=['vc'])

    NPB = 2
    p_sb = [sb(f"p_sb{i}", [128, 512], BF16) for i in range(NPB)]
    mct = [sb(f"mct{i}", [128, 512], BF16) for i in range(2)]
    o_sb = [[sb(f"o_sb{br}_{r}", [65, 512], F32) for r in range(4)] for br in range(3)]
    rec128 = sb("rec128", [128, 512], F32)
    imp = sb("imp", [128, 512], F32)
    tmp = sb("tmp", [128, 512], F32)
    score = sb("score", [128, 128], F32)
    work = sb("work", [128, 128], F32)
    m8 = sb("m8", [128, 16], F32)
    thr = sb("thr", [128, 1], F32)
    selq = sb("selq", [128, 128], BF16)
    selT = sb("selT", [128, 512], BF16)
    bsel = [sb(f"bsel{i}", [128, 128], F32) for i in range(2)]
    Mx = [sb(f"Mx{i}", [128, 512], BF16) for i in range(2)]
    gsig = sb("gsig", [12, 512], F32)
    rec64 = [sb(f"rec64_{i}", [64, 512], F32) for i in range(2)]
    t64 = [sb(f"t64_{i}", [64, 512], F32) for i in range(2)]
    acc = sb("acc", [64, 512], F32)
    outb = [sb(f"outb{i}", [64, 512], BF16) for i in range(2)]
    stepc = [0]
    zeros65 = sb("zeros65", [128, 65], BF16)
    S.add('pool', lambda e: e.memset(zeros65[:], 0.0), writes=['zeros65'])

    def run_steps(steps):
        n = len(steps)
        base = stepc[0]
        stepc[0] += n

        def qk(t):
            st = steps[t]
            bi = (base + t) % 2
            S.add('pe', lambda e: e.matmul(pb[bi][:, st['cs']], lhsT=st['k'][0], rhs=st['q'][0], start=True, stop=True),
                  reads=[st['k'][1], st['q'][1]], writes=[PB(bi)])

        def ex(t):
            st = steps[t]
            bi = (base + t) % 2
            pt = p_sb[bi]
            S.add('act', lambda e: e.activation(out=pt[:, st['cs']], in_=pb[bi][:, st['cs']], func=AF.Exp),
                  reads=[PB(bi)], writes=[f"p_sb{bi}"])
            for mi, (map_, mkey, mcs) in enumerate(st['masks']):
                eng = 'pool' if (t + mi) % 2 == 0 else 'dve'
                S.add(eng, lambda e, map_=map_, mcs=mcs: e.tensor_tensor(out=pt[:, mcs], in0=pt[:, mcs], in1=map_, op=ALU.mult),
                      reads=[f"p_sb{bi}", mkey], writes=[f"p_sb{bi}"])

        def pv(t):
            st = steps[t]
            bi = (base + t) % 2
            pt = p_sb[bi]
            for (lhs, lkey, ob, rows, start, stop) in st['outs']:
                S.add('pe', lambda e, lhs=lhs, ob=ob, rows=rows, start=start, stop=stop: e.matmul(
                    pb[ob][0:rows, st['cs']], lhsT=lhs, rhs=pt[:, st['cs']], start=start, stop=stop),
                    reads=[lkey, f"p_sb{bi}"], writes=[PB(ob)])
        if n == 0:
            return
        qk(0)
        for t in range(n):
            ex(t)
            if t + 1 < n:
                qk(t + 1)
            pv(t)

    def qslice(r, G, c0=0, c1=512):
        half = r % 2
        return nqT[r // 2][64 * half:64 * half + 64, G * 512 + c0:G * 512 + c1], f"nqT{r // 2}"

    def do_group(G):
        Q0 = G * 512
        ccmax = min(NCC - 1, (32 * G + 30) // 128)
        ccmask0 = max(0, (32 * G - 1) // 128)
        for cc in range(ccmask0, ccmax + 1):
            S.add('sp', lambda e, cc=cc: e.dma_start(out=mct[cc % 2][:], in_=mC_d[cc * 128:(cc + 1) * 128, Q0:Q0 + 512]),
                  writes=[f"mct{cc % 2}"], dma=True)
        for r in range(4):
            steps = []
            for cc in range(ccmax + 1):
                half = r % 2
                qa, qk_ = qslice(r, G)
                masks = []
                if cc >= ccmask0:
                    masks.append((mct[cc % 2][:, :], f"mct{cc % 2}", slice(0, 512)))
                steps.append(dict(k=(kcT[64 * half:64 * half + 64, cc * 128:(cc + 1) * 128], 'kcT'), q=(qa, qk_),
                                  cs=slice(0, 512), masks=masks,
                                  outs=[(vc[:, cc, :], 'vc', 2, 65, cc == 0, cc == ccmax),
                                        (Mt[:, cc, :], 'M_sb', 3, 128, cc == 0, cc == ccmax)]))
            run_steps(steps)
            ob = o_sb[0][r]
            S.add('act', lambda e, ob=ob: e.activation(out=ob[:], in_=pb[2][0:65, :], func=AF.Copy),
                  reads=[PB(2)], writes=[f"o_sb0_{r}"])
            S.add('dve', lambda e, ob=ob: e.tensor_scalar(out=ob[64:65, :], in0=ob[64:65, :], scalar1=1e-30, scalar2=None, op0=ALU.max),
                  reads=[f"o_sb0_{r}"], writes=[f"o_sb0_{r}"])
            S.add('pe', lambda e, ob=ob: e.matmul(pb[4][:, :], lhsT=sel128[:, :], rhs=ob[:, :], start=True, stop=True),
                  reads=['sel128_sb', f"o_sb0_{r}"], writes=[PB(4)])
            S.add('dve', lambda e: e.reciprocal(out=rec128[:], in_=pb[4][:, :]), reads=[PB(4)], writes=['rec128'])
            if r == 0:
                S.add('dve', lambda e: e.tensor_tensor(out=imp[:], in0=pb[3][:, :], in1=rec128[:], op=ALU.mult),
                      reads=[PB(3), 'rec128'], writes=['imp'])
            else:
                S.add('dve', lambda e: e.tensor_tensor(out=tmp[:], in0=pb[3][:, :], in1=rec128[:], op=ALU.mult),
                      reads=[PB(3), 'rec128'], writes=['tmp'])
                S.add('pool', lambda e: e.tensor_tensor(out=imp[:], in0=imp[:], in1=tmp[:], op=ALU.add),
                      reads=['imp', 'tmp'], writes=['imp'])
        for i in range(4):
            qb = 4 * G + i
            bs = bsel[i % 2]
            bk = f"bsel{i % 2}"
            S.add('sp', lambda e, bs=bs, qb=qb: e.dma_start(out=bs[:], in_=bsel_d[qb, :, :]), writes=[bk], dma=True)
            S.add('pe', lambda e, i=i: e.transpose(out=pb[0][:, 0:128], in_=imp[:, i * 128:(i + 1) * 128], identity=id32[:]),
                  reads=['imp', 'id32_sb'], writes=[PB(0)])
            S.add('dve', lambda e, bs=bs: e.tensor_tensor(out=score[:], in0=pb[0][:, 0:128], in1=bs[:], op=ALU.add),
                  reads=[PB(0), bk], writes=['score'])
            S.add('dve', lambda e: e.max(out=m8[:, 0:8], in_=score[:]), reads=['score'], writes=['m8'])
            S.add('dve', lambda e: e.match_replace(out=work[:], in_to_replace=m8[:, 0:8], in_values=score[:], imm_value=-1e30),
                  reads=['score', 'm8'], writes=['work'])
            S.add('dve', lambda e: e.max(out=m8[:, 8:16], in_=work[:]), reads=['work'], writes=['m8'])
            S.add('dve', lambda e: e.tensor_reduce(out=thr[:], in_=m8[:, 8:16], axis=AX.X, op=ALU.min), reads=['m8'], writes=['thr'])
            S.add('dve', lambda e: e.tensor_scalar(out=thr[:], in0=thr[:], scalar1=-1e8, scalar2=None, op0=ALU.max),
                  reads=['thr'], writes=['thr'])
            S.add('dve', lambda e: e.tensor_scalar(out=selq[:], in0=score[:], scalar1=thr[:, 0:1], scalar2=None, op0=ALU.is_ge),
                  reads=['score', 'thr'], writes=['selq'])
            S.add('pe', lambda e: e.transpose(out=pbf[:, 0:128], in_=selq[:], identity=idbf[:]),
                  reads=['selq', 'idbf_sb'], writes=['pbf'])
            S.add('act', lambda e, i=i: e.activation(out=selT[:, i * 128:(i + 1) * 128], in_=pbf[:, 0:128], func=AF.Copy),
                  reads=['pbf'], writes=['selT'])
        nk = 4 * G + 4
        steps = []
        for j in range(nk):
            r_ = max(0, j - 4 * G)
            c0 = r_ * 128
            mx = Mx[j % 2]
            mk = f"Mx{j % 2}"
            S.add('pe', lambda e, j=j, c0=c0: e.matmul(pb[2][:, c0:512], lhsT=Ft[:, j * 128:(j + 1) * 128], rhs=selT[:, c0:512],
                                                       start=True, stop=True), reads=['F_sb', 'selT'], writes=[PB(2)])
            S.add('act', lambda e, mx=mx, c0=c0: e.activation(out=mx[:, c0:512], in_=pb[2][:, c0:512], func=AF.Copy),
                  reads=[PB(2)], writes=[mk])
            if j >= 4 * G:
                S.add('pool', lambda e, mx=mx, c0=c0: e.tensor_tensor(out=mx[:, c0:c0 + 128], in0=mx[:, c0:c0 + 128], in1=tri[:], op=ALU.mult),
                      reads=[mk, 'tri_sb'], writes=[mk])
            steps = []
            for r in range(4):
                half = r % 2
                qa, qk_ = qslice(r, G, c0)
                steps.append(dict(k=(kslc[64 * half:64 * half + 64, j * 128:(j + 1) * 128], 'kslc'), q=(qa, qk_),
                                  cs=slice(c0, 512), masks=[(mx[:, c0:512], mk, slice(c0, 512))],
                                  outs=[(vslc[:, j, :], 'vslc_sb', 3 + r, 65, j == 0, j == nk - 1)]))
            run_steps(steps)
        for r in range(4):
            ob = o_sb[1][r]
            S.add('act', lambda e, ob=ob, r=r: e.activation(out=ob[:], in_=pb[3 + r][0:65, :], func=AF.Copy),
                  reads=[PB(3 + r)], writes=[f"o_sb1_{r}"])
        rels = [rel for rel in range(8) if 4 * G - 4 + rel >= 0]
        for r in range(4):
            S.add('pe', lambda e, r=r: e.matmul(pb[3 + r][0:65, :], lhsT=zeros65[:, :], rhs=wm[:, 0, :], start=True, stop=False),
                  reads=['zeros65', 'wm_sb'], writes=[PB(3 + r)])
        for rel in rels:
            j = 4 * G - 4 + rel
            if rel >= 4:
                c0, c1 = (rel - 4) * 128, 512
            else:
                c0, c1 = 0, (rel + 1) * 128
            steps = []
            for r in range(4):
                half = r % 2
                qa, qk_ = qslice(r, G, c0, c1)
                steps.append(dict(k=(kwin[64 * half:64 * half + 64, j * 128:(j + 1) * 128], 'kwin'), q=(qa, qk_),
                                  cs=slice(c0, c1), masks=[(wm[:, rel, c0:c1], 'wm_sb', slice(c0, c1))],
                                  outs=[(vwin[:, j, :], 'vwin_sb', 3 + r, 65, False, rel == rels[-1])]))
            run_steps(steps)
        for r in range(4):
            ob = o_sb[2][r]
            S.add('act', lambda e, ob=ob, r=r: e.activation(out=ob[:], in_=pb[3 + r][0:65, :], func=AF.Copy),
                  reads=[PB(3 + r)], writes=[f"o_sb2_{r}"])
        S.add('sp', lambda e: e.dma_start(out=gsig[:], in_=gates_d[:, Q0:Q0 + 512]), writes=['gsig'], dma=True)
        S.add('act', lambda e: e.activation(out=gsig[:], in_=gsig[:], func=AF.Sigmoid), reads=['gsig'], writes=['gsig'])
        cnt = 0
        for r in range(4):
            for br in range(3):
                ob = o_sb[br][r]
                ok = f"o_sb{br}_{r}"
                b2 = cnt % 2
                cnt += 1
                if br > 0:
                    S.add('dve', lambda e, ob=ob: e.tensor_scalar(out=ob[64:65, :], in0=ob[64:65, :], scalar1=1e-30, scalar2=None, op0=ALU.max),
                          reads=[ok], writes=[ok])
                S.add('pe', lambda e, ob=ob, b2=b2: e.matmul(pb[b2][0:64, :], lhsT=sel128[:, 0:64], rhs=ob[:, :], start=True, stop=True),
                      reads=['sel128_sb', ok], writes=[PB(b2)])
                jrow = r * 3 + br
                S.add('pe', lambda e, jrow=jrow, b2=b2: e.matmul(pb[2 + b2][0:64, :], lhsT=gsel[:, jrow * 64:(jrow + 1) * 64], rhs=gsig[:, :],
                                                                 start=True, stop=True), reads=['gsel_sb', 'gsig'], writes=[PB(2 + b2)])
                rc = rec64[b2]
                tt = t64[b2]
                S.add('dve', lambda e, rc=rc, b2=b2: e.reciprocal(out=rc[:], in_=pb[b2][0:64, :]), reads=[PB(b2)], writes=[f"rec64_{b2}"])
                S.add('dve', lambda e, rc=rc, tt=tt, ob=ob: e.tensor_tensor(out=tt[:], in0=ob[0:64, :], in1=rc[:], op=ALU.mult),
                      reads=[ok, f"rec64_{b2}"], writes=[f"t64_{b2}"])
                if br == 0:
                    S.add('dve', lambda e, tt=tt, b2=b2: e.tensor_tensor(out=acc[:], in0=tt[:], in1=pb[2 + b2][0:64, :], op=ALU.mult),
                          reads=[f"t64_{b2}", PB(2 + b2)], writes=['acc'])
                else:
                    S.add('dve', lambda e, tt=tt, b2=b2: e.tensor_tensor(out=tt[:], in0=tt[:], in1=pb[2 + b2][0:64, :], op=ALU.mult),
                          reads=[f"t64_{b2}", PB(2 + b2)], writes=[f"t64_{b2}"])
                    if br == 1:
                        S.add('pool', lambda e, tt=tt: e.tensor_tensor(out=acc[:], in0=acc[:], in1=tt[:], op=ALU.add),
                              reads=['acc', f"t64_{b2}"], writes=['acc'])
                    else:
                        obf = outb[r % 2]
                        S.add('pool', lambda e, tt=tt, obf=obf: e.tensor_tensor(out=obf[:], in0=acc[:], in1=tt[:], op=ALU.add),
                              reads=['acc', f"t64_{b2}"], writes=[f"outb{r % 2}"])
                        S.add('sp', lambda e, obf=obf, r=r: e.dma_start(out=oT_o[r * 64:(r + 1) * 64, Q0:Q0 + 512], in_=obf[:]),
                              reads=[f"outb{r % 2}"], dma=True)
    for G in range(NG):
        do_group(G)
    S.emit()
    es.close()
    return nc


BF = ml_dtypes.bfloat16
ROPE_THETA = 500000.0


def rep128(v):
    return np.ascontiguousarray(np.broadcast_to(v[None, :], (128, v.shape[0]))).astype(np.float32)


def ident_bf():
    return np.eye(128, dtype=np.float32).astype(BF)


def rope_tables(pos):
    half = 8
    inv_freq = (np.float32(ROPE_THETA) ** (-(np.arange(half, dtype=np.float32) * np.float32(2.0) / np.float32(16)))).astype(np.float32)
    ang = (pos.astype(np.float32)[:, None] * inv_freq[None, :]).astype(np.float32)
    c = np.cos(ang).astype(np.float32).T
    s = np.sin(ang).astype(np.float32).T
    N = pos.shape[0]
    cos64 = np.ones((64, N), np.float32)
    sin64 = np.zeros((64, N), np.float32)
    cos64[0:8] = c
    cos64[8:16] = c
    sin64[0:8] = -s
    sin64[8:16] = s
    return np.concatenate([cos64, cos64], 0), np.concatenate([sin64, sin64], 0)


def swap_cols(w):
    Dm, n = w.shape
    w3 = w.reshape(Dm, n // 64, 64)
    out = w3.copy()
    out[:, :, 0:8] = w3[:, :, 8:16]
    out[:, :, 8:16] = w3[:, :, 0:8]
    return out.reshape(Dm, n)


def even_w_ext(w_in):
    return np.ascontiguousarray(np.concatenate(
        [w_in, swap_cols(w_in[:, 1536:2048]), swap_cols(w_in[:, 2304:2432]), swap_cols(w_in[:, 2560:2688])], axis=1))


def conv_pack(conv_w, conv_b):
    a = np.concatenate([conv_w, conv_b[None, :]], 0)
    return np.ascontiguousarray(a.reshape(4, 22, 128).transpose(2, 1, 0)).astype(np.float32)


def nsa_consts(S_):
    f32 = np.float32
    NTL = S_ // 128
    NC = S_ // 16 - 1
    NS = S_ // 64
    c = np.arange(512)
    t = np.arange(S_)
    mC = ((16 * c[:, None] + 31 <= t[None, :]) & (c[:, None] < NC)).astype(f32)
    c_start = c * 16
    n = np.arange(128)
    sel_start = n * 64
    M = ((c_start[:, None] < sel_start[None, :] + 64) & (c_start[:, None] + 32 > sel_start[None, :]) &
         (c[:, None] < NC) & (n[None, :] < NS)).astype(f32)
    M = M.reshape(4, 128, 128).transpose(1, 0, 2)
    bsel = np.zeros((NTL, 128, 128), f32)
    for qb in range(NTL):
        tq = qb * 128 + np.arange(128)
        cur = tq // 64
        causal = (n[None, :] <= cur[:, None]) & (n[None, :] < NS)
        b = np.where(causal, 0.0, -1e9).astype(f32)
        b = np.where(causal & (n[None, :] == cur[:, None] - 1), 1e9, b)
        b = np.where(causal & (n[None, :] == cur[:, None]), 2e9, b)
        b = np.where(causal & (n[None, :] == 0), 3e9, b)
        bsel[qb] = b
    F = ((t[None, :] // 64) == n[:, None]).astype(f32)
    i = np.arange(128)
    tri = (i[:, None] <= i[None, :]).astype(f32)
    wm = np.zeros((128, 8, 512), f32)
    qc = np.arange(512)
    for rel in range(8):
        kk = (rel - 4) * 128 + i
        wm[:, rel, :] = ((kk[:, None] <= qc[None, :]) & (kk[:, None] > qc[None, :] - 512)).astype(f32)
    sel128 = np.zeros((65, 128), f32); sel128[64] = 1
    gsel = np.zeros((12, 12, 64), f32)
    for j in range(12):
        gsel[j, j, :] = 1
    posc = (16 * np.arange(512) + 31).astype(f32)
    cosC, sinC = rope_tables(posc)
    return {"M": np.ascontiguousarray(M).astype(BF), "mC": mC.astype(BF), "bsel": bsel, "F": F.astype(BF), "tri": tri.astype(BF),
            "wm": wm.astype(BF), "sel128": sel128, "gsel": gsel.reshape(12, 768), "id32": np.eye(128, dtype=f32),
            "idbf": np.eye(128, dtype=f32).astype(BF), "cosC": np.ascontiguousarray(cosC[:64]), "sinC": np.ascontiguousarray(sinC[:64])}


def nsa_weights(w_ck, w_cv, pos_k, pos_v):
    wk = w_ck.reshape(32, 64, 64).transpose(1, 0, 2).reshape(64, 2048)
    wv = w_cv.reshape(32, 64, 64).transpose(1, 0, 2).reshape(64, 2048)
    wksw = swap_cols(w_ck).reshape(32, 64, 64).transpose(1, 0, 2).reshape(64, 2048)
    pos = np.concatenate([pos_k.T, pos_v.T], 0)
    return {"wkv": np.ascontiguousarray(np.concatenate([wk, wv], 0)).astype(np.float32),
            "wksw": np.ascontiguousarray(wksw).astype(np.float32), "pos": np.ascontiguousarray(pos).astype(np.float32)}


_PROGS = {}


def _prog(key, fn):
    if key not in _PROGS:
        _PROGS[key] = fn()
    return _PROGS[key]


def _tm_aug(v):
    S_ = v.shape[0]
    a = np.concatenate([v, np.ones((S_, 1), v.dtype)], -1)
    return np.ascontiguousarray(a.reshape(S_ // 128, 128, 65).transpose(1, 0, 2))


def _tm_heads(v, nh, aug):
    S_ = v.shape[0]
    a = v.reshape(S_, nh, 64)
    if aug:
        a = np.concatenate([a, np.ones((S_, nh, 1), v.dtype)], -1)
    w = a.shape[-1]
    return np.ascontiguousarray(a.reshape(S_ // 128, 128, nh, w).transpose(2, 1, 0, 3))


def sb_consts():
    f32 = np.float32
    i = np.arange(128)
    tris = (i[:, None] < i[None, :]).astype(f32)
    negIT = -(i[:, None] >= i[None, :]).astype(f32)
    return {"tris": tris.astype(BF), "negIT": negIT.astype(BF), "negOnes": (-np.ones((128, 128), f32)).astype(BF),
            "zeros": np.zeros((128, 64), f32).astype(BF)}


def kernel(x, attn_norm, ffn_norm, ev_w_in, ev_cmp_pos_k, ev_cmp_pos_v, ev_cmp_w_k, ev_cmp_w_v, ev_w_out,
           od_w_in, od_b_f, od_w_out, ffn_w_in, ffn_conv_w, ffn_conv_b, ffn_w_out, final_norm):
    f32 = np.float32
    x = np.asarray(x, f32)
    B, S_, Dm = x.shape
    NT = S_ // 2
    NCORE = 2 * B
    DEPTH = attn_norm.shape[0]
    cores = list(range(NCORE))
    X = np.ascontiguousarray(x.reshape(B * S_, Dm))
    cosT, sinT = rope_tables(np.arange(S_, dtype=f32))
    ident = ident_bf()
    sbc = sb_consts()
    nsac = nsa_consts(S_)
    i128 = np.arange(128)
    tri_le = (i128[:, None] <= i128[None, :]).astype(f32).astype(BF)
    sel64 = np.zeros((65, 64), f32)
    sel64[64] = 1

    def next_inputs(layer, c):
        half = c % 2
        if layer % 2 == 0:
            e = layer // 2
            return {"g1": rep128(np.asarray(attn_norm[layer], f32)), "w_i": even_w_ext(np.asarray(ev_w_in[e], f32)),
                    "cosT": np.ascontiguousarray(cosT[:, half * NT:(half + 1) * NT]),
                    "sinT": np.ascontiguousarray(sinT[:, half * NT:(half + 1) * NT])}
        o = layer // 2
        return {"g1": rep128(np.asarray(attn_norm[layer], f32)), "w_i": np.ascontiguousarray(np.asarray(od_w_in[o], f32))}

    def gather(res, name, axis):
        out = []
        for b in range(B):
            out.append(np.concatenate([res.results[2 * b][name], res.results[2 * b + 1][name]], axis=axis))
        return out

    nc = _prog(('dense', NT, False, 'even', False), lambda: build_dense(NT, False, 'even', False))
    in_maps = []
    for c in cores:
        m = {"x_in": X[c * NT:(c + 1) * NT], "ident": ident}
        m.update(next_inputs(0, c))
        in_maps.append(m)
    res = run_bass_kernel_spmd(nc, in_maps, core_ids=cores)
    FM = gather(res, "fm", 1)
    FM32 = gather(res, "fm32", 1)
    TM = gather(res, "tm", 0)

    y = None
    for layer in range(DEPTH):
        OT = []
        if layer % 2 == 0:
            e = layer // 2
            ncs = _prog(('sb', S_), lambda: build_sb(S_, 4))
            in_maps = []
            for c in cores:
                b, p = c // 2, c % 2
                m = {"qT": np.ascontiguousarray(FM[b][p * 256:(p + 1) * 256].reshape(2, 128, S_)),
                     "kT": np.ascontiguousarray(FM[b][512 + p * 256:512 + (p + 1) * 256].reshape(2, 128, S_)),
                     "v": _tm_heads(TM[b][:, p * 256:(p + 1) * 256], 4, False)}
                m.update(sbc)
                in_maps.append(m)
            rs = run_bass_kernel_spmd(ncs, in_maps, core_ids=cores)
            ncn = _prog(('nsa', S_), lambda: build_nsa(S_))
            nw = nsa_weights(np.asarray(ev_cmp_w_k[e], f32), np.asarray(ev_cmp_w_v[e], f32),
                             np.asarray(ev_cmp_pos_k[e], f32), np.asarray(ev_cmp_pos_v[e], f32))
            in_maps = []
            for c in cores:
                b, g = c // 2, c % 2
                kc = FM[b][1536 + g * 64:1536 + (g + 1) * 64]
                vcm = FM[b][1664 + g * 64:1664 + (g + 1) * 64]
                ks = FM[b][1792 + g * 64:1792 + (g + 1) * 64]
                kw = FM[b][1920 + g * 64:1920 + (g + 1) * 64]
                m = {"nqT": np.ascontiguousarray(FM[b][1024 + g * 256:1024 + (g + 1) * 256].reshape(2, 128, S_)),
                     "kvcmpT": np.ascontiguousarray(np.concatenate([kc, vcm], 0)),
                     "kslcT2": np.ascontiguousarray(np.concatenate([ks, ks], 0)),
                     "kwinT2": np.ascontiguousarray(np.concatenate([kw, kw], 0)),
                     "vslc": _tm_aug(TM[b][:, 512 + g * 64:512 + (g + 1) * 64]),
                     "vwin": _tm_aug(TM[b][:, 640 + g * 64:640 + (g + 1) * 64]),
                     "gatesT": np.ascontiguousarray(FM32[b][g * 12:(g + 1) * 12])}
                m.update(nsac)
                m.update(nw)
                in_maps.append(m)
            rn = run_bass_kernel_spmd(ncn, in_maps, core_ids=cores)
            for b in range(B):
                OT.append(np.concatenate([rs.results[2 * b]["oT"], rs.results[2 * b + 1]["oT"],
                                          rn.results[2 * b]["oT"], rn.results[2 * b + 1]["oT"]], 0))
            w_o = np.asarray(ev_w_out[e], f32)
        else:
            o = layer // 2
            ncf = _prog(('fox', S_), lambda: build_fox(S_, 8))
            bfv = np.asarray(od_b_f[o], f32)
            in_maps = []
            for c in cores:
                b, p = c // 2, c % 2
                m = {"qT": np.ascontiguousarray(FM[b][p * 512:(p + 1) * 512].reshape(8, 64, S_)),
                     "kT": np.ascontiguousarray(FM[b][1024 + p * 512:1024 + (p + 1) * 512].reshape(8, 64, S_)),
                     "vaug": _tm_heads(TM[b][:, p * 512:(p + 1) * 512], 8, True),
                     "fT": np.ascontiguousarray(FM32[b][8 * p:8 * p + 8]),
                     "bf": np.ascontiguousarray(bfv[8 * p:8 * p + 8, None]),
                     "tri": tri_le, "sel64": sel64}
                in_maps.append(m)
            rf = run_bass_kernel_spmd(ncf, in_maps, core_ids=cores)
            for b in range(B):
                OT.append(np.concatenate([rf.results[2 * b]["oT"], rf.results[2 * b + 1]["oT"]], 0))
            w_o = np.asarray(od_w_out[o], f32)
        last = (layer == DEPTH - 1)
        nk = None if last else ('even' if (layer + 1) % 2 == 0 else 'odd')
        ncd = _prog(('dense', NT, True, nk, last), lambda: build_dense(NT, True, nk, last))
        in_maps = []
        for c in cores:
            b, half = c // 2, c % 2
            t0 = c * NT
            if half == 0:
                xin = np.concatenate([np.zeros((128, Dm), f32), X[t0:t0 + NT]], 0)
                oT = np.concatenate([np.zeros((Dm, 128), BF), OT[b][:, 0:NT]], 1)
            else:
                xin = X[t0 - 128:t0 + NT]
                oT = OT[b][:, NT - 128:2 * NT]
            m = {"x_in": np.ascontiguousarray(xin), "ident": ident, "oT": np.ascontiguousarray(oT), "w_o": np.ascontiguousarray(w_o),
                 "g2": rep128(np.asarray(ffn_norm[layer], f32)), "w_fi": np.ascontiguousarray(np.asarray(ffn_w_in[layer], f32)),
                 "w_fo": np.ascontiguousarray(np.asarray(ffn_w_out[layer], f32)),
                 "conv_w": conv_pack(np.asarray(ffn_conv_w[layer], f32), np.asarray(ffn_conv_b[layer], f32)),
                 "flag": np.full((128, 1), float(half), f32)}
            if last:
                m["gF"] = rep128(np.asarray(final_norm, f32))
            else:
                m.update(next_inputs(layer + 1, c))
            in_maps.append(m)
        rd = run_bass_kernel_spmd(ncd, in_maps, core_ids=cores)
        if last:
            y = np.concatenate([rd.results[c]["y"] for c in cores], 0)
        else:
            X = np.ascontiguousarray(np.concatenate([rd.results[c]["x_out"] for c in cores], 0))
            FM = gather(rd, "fm", 1)
            FM32 = gather(rd, "fm32", 1)
            TM = gather(rd, "tm", 0)
    return y.reshape(B, S_, Dm).astype(np.float32)
```
